# Optimizing a Trainium2 kernel written in Bass

```python
import jax, jax.numpy as jnp
from jax import lax
import numpy as np

D_MODEL = 2048
BATCH = 8
SEQ = 4096
DEPTH = 1
DEC_BATCH = 16
DEC_SEQ = 64
PAST_LEN = 2048

CHUNK = 64
N_PAST_CHUNKS = 8
ATTN_WINDOW = N_PAST_CHUNKS * CHUNK
BAND = ATTN_WINDOW + CHUNK
D_ATTN = D_MODEL // 2
HEAD_DIM = 128
N_HEADS_A = D_ATTN // HEAD_DIM
D_GMLP = D_MODEL - D_ATTN
N_GROUPS_B = 8
GROUP_DIM_B = D_GMLP // N_GROUPS_B
GMLP_CHUNK = 128
REL_CLIP = 128
D_FF = ((8 * D_MODEL // 3 + 127) // 128) * 128
CONV_W = 3
D_IN = 3 * D_ATTN + 2 * D_GMLP
EPS = 1e-6
NEG_INF = -1e30

kernel_name = "hymba_chunk_attn_gmlp_convffn_step"


def rms_norm(x, g):
    xf = x.astype(jnp.float32)
    y = xf * lax.rsqrt(jnp.mean(xf * xf, axis=-1, keepdims=True) + EPS)
    return (y * g.astype(jnp.float32)).astype(x.dtype)


def layer_norm(x, g, b):
    xf = x.astype(jnp.float32)
    mu = jnp.mean(xf, axis=-1, keepdims=True)
    var = jnp.mean(jnp.square(xf - mu), axis=-1, keepdims=True)
    y = (xf - mu) * lax.rsqrt(var + EPS)
    return (y * g.astype(jnp.float32) + b.astype(jnp.float32)).astype(x.dtype)


def rel_bias(table, q_pos, k_pos):
    rel = jnp.clip(q_pos[:, None] - k_pos[None, :], -REL_CLIP, REL_CLIP) + REL_CLIP
    return table[:, rel].astype(jnp.float32)


def attend(q, k, v, bias, mask):
    s = jnp.einsum('bqhd,bkhd->bhqk', q, k).astype(jnp.float32) * (HEAD_DIM ** -0.5) + bias[None]
    if mask is not None:
        s = jnp.where(mask, s, NEG_INF)
    p = jax.nn.softmax(s, axis=-1).astype(v.dtype)
    return jnp.einsum('bhqk,bkhd->bqhd', p, v)


def mix_inputs(n, w_in, q_g, k_g, ln_g, ln_b):
    B, T, _ = n.shape
    z = n @ w_in
    q, k, va, u, vb = jnp.split(z, [D_ATTN, 2 * D_ATTN, 3 * D_ATTN, 3 * D_ATTN + D_GMLP], axis=-1)
    q = rms_norm(q.reshape(B, T, N_HEADS_A, HEAD_DIM), q_g)
    k = rms_norm(k.reshape(B, T, N_HEADS_A, HEAD_DIM), k_g)
    va = va.reshape(B, T, N_HEADS_A, HEAD_DIM)
    u = jax.nn.gelu(u, approximate=False)
    vg = jax.nn.gelu(vb, approximate=False).reshape(B, T, N_GROUPS_B, GROUP_DIM_B)
    vn = layer_norm(vg, ln_g, ln_b)
    return q, k, va, u, vn


def spatial_gate(u, vn, w_s, b_s):
    L = vn.shape[2]
    tri = jnp.tril(jnp.ones((L, L), dtype=bool))
    ws = jnp.where(tri[None], w_s[:, :L, :L], 0.0).astype(vn.dtype)
    vs = jnp.einsum('gts,bnsgc->bntgc', ws, vn) + b_s[:, :L].T[None, None, :, :, None]
    return u * vs.reshape(u.shape)


def causal_dwconv(prev, a, w, b):
    T = a.shape[1]
    ap = jnp.concatenate([prev, a], axis=1)
    out = b + sum(w[i] * ap[:, i:i + T] for i in range(CONV_W))
    return out, ap[:, ap.shape[1] - (CONV_W - 1):]


def conv_ffn(h, prev, g, w_up, cw, cb, w_down):
    n = rms_norm(h, g)
    a, gv = jnp.split(n @ w_up, 2, axis=-1)
    ac, new_prev = causal_dwconv(prev, a, cw, cb)
    return h + (jax.nn.silu(ac) * gv) @ w_down, new_prev


def prompt_layer(x, nmg, w_in, qg, kg, tab, lng, lnb, ws, bs, w_out, nfg, w_up, cw, cb, w_down):
    B, S, _ = x.shape
    q, k, va, u, vn = mix_inputs(rms_norm(x, nmg), w_in, qg, kg, lng, lnb)
    nc = S // CHUNK
    pad = ((0, 0), (ATTN_WINDOW, 0), (0, 0), (0, 0))
    kp = jnp.pad(k, pad)
    vp = jnp.pad(va, pad)
    li = jnp.arange(CHUNK)
    lj = jnp.arange(BAND)
    bias = rel_bias(tab, li + ATTN_WINDOW, lj)
    qc = jnp.moveaxis(q.reshape(B, nc, CHUNK, N_HEADS_A, HEAD_DIM), 1, 0)

    def one_chunk(args):
        qb, c = args
        kb = lax.dynamic_slice_in_dim(kp, c * CHUNK, BAND, axis=1)
        vb = lax.dynamic_slice_in_dim(vp, c * CHUNK, BAND, axis=1)
        valid = (c * CHUNK - ATTN_WINDOW + lj) >= 0
        return attend(qb, kb, vb, bias, valid[None, None, None, :])

    oa = lax.map(one_chunk, (qc, jnp.arange(nc)))
    oa = jnp.moveaxis(oa, 0, 1).reshape(B, S, D_ATTN)
    ob = spatial_gate(u, vn.reshape(B, S // GMLP_CHUNK, GMLP_CHUNK, N_GROUPS_B, GROUP_DIM_B), ws, bs)
    h = x + jnp.concatenate([oa, ob], axis=-1) @ w_out
    prev0 = jnp.zeros((B, CONV_W - 1, D_FF), dtype=x.dtype)
    y, conv_state = conv_ffn(h, prev0, nfg, w_up, cw, cb, w_down)
    keep = min(ATTN_WINDOW, S)
    return y, k[:, S - keep:], va[:, S - keep:], conv_state


def sample_layer(x, ck, cv, cst, nmg, w_in, qg, kg, tab, lng, lnb, ws, bs, w_out, nfg, w_up, cw, cb, w_down):
    B, T, _ = x.shape
    W = ck.shape[1]
    q, k, va, u, vn = mix_inputs(rms_norm(x, nmg), w_in, qg, kg, lng, lnb)
    q_pos = PAST_LEN + jnp.arange(T)
    k_pos = jnp.concatenate([PAST_LEN - W + jnp.arange(W), q_pos])
    k_all = jnp.concatenate([ck, k], axis=1)
    v_all = jnp.concatenate([cv, va], axis=1)
    oa = attend(q, k_all, v_all, rel_bias(tab, q_pos, k_pos), None).reshape(B, T, D_ATTN)
    ob = spatial_gate(u, vn.reshape(B, 1, T, N_GROUPS_B, GROUP_DIM_B), ws, bs)
    h = x + jnp.concatenate([oa, ob], axis=-1) @ w_out
    y, conv_state = conv_ffn(h, cst, nfg, w_up, cw, cb, w_down)
    return y, k, va, vn.reshape(B, T, D_GMLP), conv_state


def setup_inputs(seed: int = 0) -> dict:
    key = jax.random.key(seed)
    ks = jax.random.split(key, 24)
    f = jnp.float32
    nrm = lambda i, shape, s: (jax.random.normal(ks[i], shape, f) * s)
    w_cache = min(ATTN_WINDOW, PAST_LEN)
    return {
        "x_prompt": nrm(0, (BATCH, SEQ, D_MODEL), 1.0),
        "x_sample": nrm(1, (DEC_BATCH, DEC_SEQ, D_MODEL), 1.0),
        "cache_attn_k": nrm(2, (DEPTH, DEC_BATCH, w_cache, N_HEADS_A, HEAD_DIM), 1.0),
        "cache_attn_v": nrm(3, (DEPTH, DEC_BATCH, w_cache, N_HEADS_A, HEAD_DIM), 1.0),
        "state_ffn_conv": nrm(4, (DEPTH, DEC_BATCH, CONV_W - 1, D_FF), 1.0),
        "norm_mix_g": 1.0 + nrm(5, (DEPTH, D_MODEL), 0.02),
        "w_in": nrm(6, (DEPTH, D_MODEL, D_IN), D_MODEL ** -0.5),
        "q_norm_g": 1.0 + nrm(7, (DEPTH, HEAD_DIM), 0.02),
        "k_norm_g": 1.0 + nrm(8, (DEPTH, HEAD_DIM), 0.02),
        "rel_bias_table": nrm(9, (DEPTH, N_HEADS_A, 2 * REL_CLIP + 1), 0.1),
        "gmlp_ln_g": 1.0 + nrm(10, (DEPTH, GROUP_DIM_B), 0.02),
        "gmlp_ln_b": nrm(11, (DEPTH, GROUP_DIM_B), 0.02),
        "gmlp_w_s": nrm(12, (DEPTH, N_GROUPS_B, GMLP_CHUNK, GMLP_CHUNK), GMLP_CHUNK ** -0.5),
        "gmlp_b_s": 1.0 + nrm(13, (DEPTH, N_GROUPS_B, GMLP_CHUNK), 0.02),
        "w_out": nrm(14, (DEPTH, D_MODEL, D_MODEL), D_MODEL ** -0.5),
        "norm_ffn_g": 1.0 + nrm(15, (DEPTH, D_MODEL), 0.02),
        "w_up": nrm(16, (DEPTH, D_MODEL, 2 * D_FF), D_MODEL ** -0.5),
        "ffn_conv_w": nrm(17, (DEPTH, CONV_W, D_FF), CONV_W ** -0.5),
        "ffn_conv_b": nrm(18, (DEPTH, D_FF), 0.02),
        "w_down": nrm(19, (DEPTH, D_FF, D_MODEL), D_FF ** -0.5),
    }


def reference(x_prompt, x_sample, cache_attn_k, cache_attn_v, state_ffn_conv,
              norm_mix_g, w_in, q_norm_g, k_norm_g, rel_bias_table, gmlp_ln_g, gmlp_ln_b,
              gmlp_w_s, gmlp_b_s, w_out, norm_ffn_g, w_up, ffn_conv_w, ffn_conv_b, w_down):
    xp, xs = x_prompt, x_sample
    pk, pv, pc, sk, sv, sg, sc = [], [], [], [], [], [], []
    for l in range(DEPTH):
        w = (norm_mix_g[l], w_in[l], q_norm_g[l], k_norm_g[l], rel_bias_table[l], gmlp_ln_g[l],
             gmlp_ln_b[l], gmlp_w_s[l], gmlp_b_s[l], w_out[l], norm_ffn_g[l], w_up[l],
             ffn_conv_w[l], ffn_conv_b[l], w_down[l])
        xp, k_p, v_p, c_p = prompt_layer(xp, *w)
        xs, k_s, v_s, g_s, c_s = sample_layer(xs, cache_attn_k[l], cache_attn_v[l], state_ffn_conv[l], *w)
        pk.append(k_p); pv.append(v_p); pc.append(c_p)
        sk.append(k_s); sv.append(v_s); sg.append(g_s); sc.append(c_s)
    return (xp, xs, jnp.stack(pk), jnp.stack(pv), jnp.stack(pc),
            jnp.stack(sk), jnp.stack(sv), jnp.stack(sg), jnp.stack(sc))
```

```python
import numpy as np
from contextlib import ExitStack
import concourse.bass as bass
import concourse.mybir as mybir
from concourse.bass_utils import run_bass_kernel_spmd

F32 = mybir.dt.float32
BF16 = mybir.dt.bfloat16
AF = mybir.ActivationFunctionType
ALU = mybir.AluOpType

D = 2048
KC = 16
DIN = 5120
DFF = 5504
NF = 43
NH = 8
NT = 3
T = NT * 128
SCALE = 128.0 ** -0.5
EPS = 1e-6
NEG = -30000.0
N_CORES = 8
import os
STOP = int(os.environ.get('MK_STOP', '99'))
NBMAX = int(os.environ.get('MK_NB', '10'))
STOPP = int(os.environ.get('MK_STOPP', '0'))
PREFETCH = int(os.environ.get('MK_PREFETCH', '1'))
NCONV_MAX = int(os.environ.get('MK_NCONV', '1'))
VMODE = int(os.environ.get('MK_VMODE', '3'))


class Buf:
    __slots__ = ("name", "w", "r", "rd", "al", "sem", "cnt", "excl")

    def __init__(self, name):
        self.name = name
        self.w = None
        self.r = {}
        self.rd = []
        self.al = []
        self.sem = None
        self.cnt = 0
        self.excl = False


class Instr:
    __slots__ = ("eng", "fn", "dma", "deps", "signal", "sem", "val", "key")

    def __init__(self, eng, fn, dma):
        self.eng = eng
        self.fn = fn
        self.dma = dma
        self.deps = []
        self.signal = False
        self.sem = None
        self.val = 0
        self.key = None


class Prog:
    ENG = ("pe", "act", "dve", "pool", "sp")

    def __init__(self, nc, es):
        self.nc = nc
        self.es = es
        self.e = {"pe": nc.tensor, "act": nc.scalar, "dve": nc.vector, "pool": nc.gpsimd, "sp": nc.sync}
        self.ins = []
        self.keybufs = []

    def _dep(self, I, reads, writes):
        cand = {}

        def add(Dp, raw):
            if Dp is None or Dp is I:
                return
            k = id(Dp)
            if k in cand:
                cand[k] = (Dp, cand[k][1] or raw)
            else:
                cand[k] = (Dp, raw)

        for b in reads:
            add(b.w, True)
            if b.excl:
                for rr in b.r.values():
                    if rr.eng != I.eng:
                        add(rr, False)
        for b in writes:
            for x in [b] + b.al:
                add(x.w, False)
                for rr in x.r.values():
                    add(rr, False)
                for rr in x.rd:
                    add(rr, False)
        for Dp, raw in cand.values():
            if (not I.dma) and (not Dp.dma) and Dp.eng == I.eng and I.eng == "pe":
                continue
            I.deps.append(Dp)
            Dp.signal = True
        for b in reads:
            if I.dma:
                b.rd.append(I)
            else:
                b.r[I.eng] = I
        for b in writes:
            for x in [b] + b.al:
                x.w = I
                x.r = {}
                x.rd = []

    def op(self, eng, fn, r=(), w=()):
        I = Instr(eng, fn, False)
        self._dep(I, r, w)
        self.ins.append(I)
        return I

    def dma(self, eng, out, in_, r=(), w=(), key=None):
        ee = self.e[eng]
        I = Instr(eng, lambda: ee.dma_start(out=out, in_=in_), True)
        I.key = key if key is not None else (w[0] if w else r[0])
        self._dep(I, r, w)
        self.ins.append(I)
        return I

    def finalize(self):
        nc, es = self.nc, self.es
        esem = {}
        for e in self.ENG:
            esem[e] = (es.enter_context(nc.semaphore("c_" + e)), "c_" + e)
        cnt = {e: 0 for e in self.ENG}
        nd = 0
        for I in self.ins:
            if I.dma:
                b = I.key
                kind = 1 if I.eng == "pool" else 0
                if b.sem is None:
                    b.sem = {}
                    b.cnt = {}
                if kind not in b.sem:
                    b.sem[kind] = (es.enter_context(nc.semaphore("d%d_%s" % (nd, b.name))), "d%d" % nd)
                    b.cnt[kind] = 0
                    nd += 1
                    self.keybufs.append((b, kind))
                b.cnt[kind] += 16
                I.sem = b.sem[kind]
                I.val = b.cnt[kind]
            elif I.signal:
                cnt[I.eng] += 1
                I.sem = esem[I.eng]
                I.val = cnt[I.eng]
        seen = {e: {} for e in self.ENG}
        nwait = 0
        for I in self.ins:
            ee = self.e[I.eng]
            need = {}
            for Dp in I.deps:
                k = Dp.sem[1]
                if k not in need or need[k][1] < Dp.val:
                    need[k] = (Dp.sem[0], Dp.val)
            sn = seen[I.eng]
            for k, (sem, val) in need.items():
                if sn.get(k, 0) < val:
                    ee.wait_ge(sem, val)
                    sn[k] = val
                    nwait += 1
            x = I.fn()
            if I.dma:
                x.then_inc(I.sem[0], 16)
            elif I.signal:
                x.then_inc(I.sem[0], 1)
        for b, kind in self.keybufs:
            nc.sync.wait_ge(b.sem[kind][0], b.cnt[kind])
        for e in ("pe", "act", "dve", "pool"):
            if cnt[e] > 0:
                nc.sync.wait_ge(esem[e][0], cnt[e])
        return len(self.ins), nwait, nd


class Ring:
    def __init__(self, nc, es, name, shape, dtype, n):
        self.t = [es.enter_context(nc.sbuf_tensor("r_%s%d" % (name, i), shape, dtype)) for i in range(n)]
        self.b = [Buf("%s%d" % (name, i)) for i in range(n)]
        self.i = 0

    def next(self):
        k = self.i % len(self.t)
        self.i += 1
        return self.t[k], self.b[k]

    @classmethod
    def views(cls, pairs):
        r = cls.__new__(cls)
        r.t = [a for a, _ in pairs]
        r.b = [b for _, b in pairs]
        r.i = 0
        return r


def build(NP):
    assert (NP + 1) % NT == 0 and NP >= 4
    NPASS = (NP + 1) // NT
    SEQ = NP * 128
    KEEP0 = NP - 4

    nc = bass.Bass("TRN2", target_bir_lowering=False)

    def din(name, shape, dt=F32):
        return nc.dram_tensor(name, list(shape), dt, kind="ExternalInput").ap()

    def dout(name, shape):
        return nc.dram_tensor(name, list(shape), F32, kind="ExternalOutput").ap()

    xp = din("xp", [SEQ, D])
    xs = din("xs", [128, D])
    ck = din("ck", [2, 512, 1024])
    cv = din("cv", [2, 512, 1024])
    cst = din("cst", [2, 128, NF, 2])
    w_in = din("w_in", [D, DIN])
    w_out = din("w_out", [D, D])
    w_up = din("w_up", [D, 2 * DFF])
    w_down = din("w_down", [DFF, D])
    g1col_d = din("g1col", [128, KC])
    g2col_d = din("g2col", [128, KC])
    gqb_d = din("gqb", [128, 128])
    gkb_d = din("gkb", [128, 128])
    tpadR_d = din("tpadR", [NH, 128 * 384])
    ch_d = din("ch", [128, NH])
    lngb_d = din("lngb", [128, 128])
    lnbb_d = din("lnbb", [128, 128])
    ws_d = din("ws", [128, NH, 128])
    bsb_d = din("bsb", [128, NH, 128])
    convw_d = din("convw", [128, NF, 3])
    convb_d = din("convb", [128, NF])

    yp = dout("yp", [SEQ, D])
    ys = dout("ys", [128, D])
    nkp = dout("nkp", [512, 1024])
    nvp = dout("nvp", [512, 1024])
    ncp = dout("ncp", [2, DFF])
    nks = dout("nks", [128, 1024])
    nvs = dout("nvs", [128, 1024])
    ngs = dout("ngs", [128, 1024])
    ncs = dout("ncs", [2, 2, DFF])

    NBLK = 48
    scr = nc.dram_tensor("scr", [NBLK, 128, KC * 512], BF16, kind="Internal").ap()

    es = ExitStack()
    P = Prog(nc, es)
    E = es.enter_context

    def sb(name, shape, dt=F32):
        return E(nc.sbuf_tensor("s_" + name, list(shape), dt))

    NXH = 3
    xh = Ring(nc, es, "xh", [128, D], F32, NXH)
    kT = sb("kT", [128, NH, 9 * 128], BF16)
    kTb = [Buf("kT%d" % i) for i in range(9)]
    Va = sb("Va", [128, 9, NH, 130], BF16)
    Vab = [Buf("Va%d" % i) for i in range(9)]
    BT = sb("BT", [128, NH, 256])
    BTb = Buf("BT")
    chT = sb("chT", [128, NH]); chb = Buf("ch")
    bsb = sb("bsb", [128, NH, 128]); bsbb = Buf("bsb")
    wsT = sb("wsT", [128, NH, 128], BF16); wsTb = Buf("wsT")
    wsTS = sb("wsTS", [128, NH, 128], BF16); wsTSb = Buf("wsTS")
    gqb = sb("gqb", [128, 128]); gqbb = Buf("gqb")
    gkb = sb("gkb", [128, 128]); gkbb = Buf("gkb")
    lngb = sb("lngb", [128, 128]); lngbb = Buf("lngb")
    lnbb = sb("lnbb", [128, 128]); lnbbb = Buf("lnbb")
    g1col = sb("g1col", [128, KC]); g1b = Buf("g1col")
    g2col = sb("g2col", [128, KC]); g2b = Buf("g2col")
    convw = sb("convw", [128, NF, 3]); convwb = Buf("convw")
    convb = sb("convb", [128, NF]); convbb = Buf("convb")
    ident = sb("ident", [128, 128], BF16); identb = Buf("ident")
    identf = sb("identf", [128, 128]); identfb = Buf("identf")
    mhalf = sb("mhalf", [128, 4]); mhalfb = Buf("mhalf")
    mku = sb("mku", [1, 128], BF16); mkv = sb("mkv", [1, 128], BF16); mkb = Buf("mkuv")
    halo = sb("halo", [128, NF, 2]); halob = Buf("halo")
    haloS = sb("haloS", [128, 2, NF, 2]); haloSb = Buf("haloS")
    aconv = sb("aconv", [128, 3, 2, NF]); aconvb = Buf("aconv")

    NSLOT = 3
    wslot = [sb("wslot%d" % i, [128, KC, 512], BF16) for i in range(NSLOT)]
    wslotb = [Buf("wslot%d" % i) for i in range(NSLOT)]
    scrb = [Buf("scr%d" % i) for i in range(NBLK)]

    nmix = sb("nmix", [128, KC, T], BF16)
    nmixb = [[Buf("nmix%d_%d" % (t, hf)) for hf in range(2)] for t in range(NT)]

    arena = sb("arena", [128, NF * T], BF16)
    mT = arena[:, :].rearrange("p (f t) -> p f t", t=T)
    mTb = [Buf("mT%d" % f) for f in range(NF)]
    qT = arena[:, 0:NH * T].rearrange("p (h t) -> p h t", t=T)
    qTb = [Buf("qT%d" % t) for t in range(NT)]
    uT = arena[:, NH * T:2 * NH * T].rearrange("p (c t) -> p c t", t=T)
    uTb = Buf("uT")
    vn = arena[:, 2 * NH * T:2 * NH * T + NT * 1024].rearrange("p (t c) -> p t c", c=1024)
    vnb = [Buf("vn%d" % t) for t in range(NT)]
    for f in range(NF):
        lo, hi = f * T, (f + 1) * T
        al = []
        if lo < NH * T:
            al += qTb
        if lo < 2 * NH * T and hi > NH * T:
            al.append(uTb)
        if lo < 2 * NH * T + NT * 1024 and hi > 2 * NH * T:
            al += vnb
        for x in al:
            mTb[f].al.append(x)
            x.al.append(mTb[f])

    nbR = Ring(nc, es, "nb", [128, D], BF16, 1)
    ss1R = Ring(nc, es, "ss1", [128, 4], F32, 6)
    rs1R = Ring(nc, es, "rs1", [128, 4], F32, 6)
    qnR = Ring(nc, es, "qn", [128, 512], BF16, 2)
    _w, wsT2b = qnR.next()
    wsT2 = _w[:, :].rearrange("p (g t) -> p g t", t=64)
    st32R = Ring(nc, es, "st32", [128, 512], F32, 2)
    scr8 = sb("scr8", [128, D])
    xtmp = scr8[:, :]
    xtmpb = Buf("xtmp")
    _gvb = [Buf("gv0"), Buf("gv1")]
    _gtb = Buf("gtmp0")
    gvR = Ring.views([(scr8[:, 1024:1536], _gvb[0]), (scr8[:, 1536:2048], _gvb[1])])
    for _b in _gvb + [_gtb]:
        _b.al.append(xtmpb)
        xtmpb.al.append(_b)
    bnsR = Ring(nc, es, "bns", [128, 4, 6], F32, 3)
    bnmR = Ring(nc, es, "bnm", [128, 4, 2], F32, 3)
    stmpR = Ring(nc, es, "stmp", [128, 256], F32, 3)
    PTR = Ring(nc, es, "PT", [128, 640], BF16, 3)
    recR = Ring(nc, es, "rec", [128, 4], F32, 4)
    oaR = Ring(nc, es, "oa", [128, 1024], BF16, 2)
    gtmpR = Ring.views([(scr8[:, 0:1024], _gtb)])
    _gt1 = Buf("gtmp1")
    for _b in _gvb + [xtmpb]:
        _b.al.append(_gt1)
        _gt1.al.append(_b)
    cstR = Ring.views([(scr8[:, 0:1024], _gtb), (scr8[:, 1024:2048], _gt1)])
    accR = Ring(nc, es, "acc", [128, T], F32, 2)

    psum = [E(nc.psum_tensor("ps%d" % i, [128, 512], F32)) for i in range(8)]
    psumb = [Buf("ps%d" % i) for i in range(8)]
    for _b in psumb:
        _b.excl = True
    pctr = [0]

    def bank():
        k = pctr[0] % 8
        pctr[0] += 1
        return psum[k], psumb[k]

    def load_const(t, src, b):
        P.dma("sp", t[:], src, w=[b])

    load_const(g1col, g1col_d, g1b)
    load_const(g2col, g2col_d, g2b)
    load_const(gqb, gqb_d, gqbb)
    load_const(gkb, gkb_d, gkbb)
    load_const(chT, ch_d, chb)
    load_const(lngb, lngb_d, lngbb)
    load_const(lnbb, lnbb_d, lnbbb)
    load_const(bsb, bsb_d, bsbb)
    load_const(convw, convw_d, convwb)
    load_const(convb, convb_d, convbb)
    for h in range(NH):
        src = bass.AP(tensor=tpadR_d.tensor, offset=h * 128 * 384 + 128, ap=[[383, 128], [1, 256]])
        P.dma("sp", BT[:, h, :], src, w=[BTb])
    P.op("pool", lambda: nc.gpsimd.memset(BT[64:128, :, 0:64], NEG), w=[BTb])
    P.op("pool", lambda: nc.gpsimd.memset(mhalf[:], -0.5), w=[mhalfb])
    P.op("pool", lambda: nc.gpsimd.memset(mku[:], 0.0), w=[mkb])
    P.op("pool", lambda: nc.gpsimd.memset(mku[:, 0:64], 1.0), w=[mkb])
    P.op("pool", lambda: nc.gpsimd.memset(mkv[:], 0.0), w=[mkb])
    P.op("pool", lambda: nc.gpsimd.memset(mkv[:, 64:128], NEG), w=[mkb])
    P.op("pool", lambda: nc.gpsimd.memset(identf[:], 1.0), w=[identfb])
    P.op("pool", lambda: nc.gpsimd.affine_select(out=identf[:], in_=identf[:], pattern=[[-1, 128]],
                                                 compare_op=ALU.is_equal, fill=0.0, base=0,
                                                 channel_multiplier=1), r=[identfb], w=[identfb])
    P.op("pool", lambda: nc.gpsimd.tensor_copy(out=ident[:], in_=identf[:]), r=[identfb], w=[identb])
    P.op("pool", lambda: nc.gpsimd.memset(halo[:], 0.0), w=[halob])
    for _k in range(len(PTR.t)):
        P.op("pool", (lambda _k=_k: nc.gpsimd.memset(PTR.t[_k][:], 0.0)), w=[PTR.b[_k]])
    for s in range(9):
        P.op("pool", (lambda s=s: nc.gpsimd.memset(Va[:, s, :, 128:129], 1.0)), w=[Vab[s]])
    _w, wsmb = gtmpR.next()
    wsm = _w[:, :].rearrange("p (g s) -> p g s", s=128)
    _w, wsm16b = oaR.next()
    wsm16 = _w[:, :].rearrange("p (g s) -> p g s", s=128)
    _w, wsm16sb = oaR.next()
    wsm16s = _w[0:64, :].rearrange("p (g s) -> p g s", s=128)
    P.dma("sp", wsm, ws_d, w=[wsmb])
    P.op("pool", lambda: nc.gpsimd.affine_select(out=wsm, in_=wsm, pattern=[[0, NH], [-1, 128]],
                                                 compare_op=ALU.is_ge, fill=0.0, base=0,
                                                 channel_multiplier=1), r=[wsmb], w=[wsmb])
    P.op("pool", lambda: nc.gpsimd.tensor_copy(out=wsm16, in_=wsm), r=[wsmb], w=[wsm16b])
    P.op("pool", lambda: nc.gpsimd.memset(wsm16s, 0.0), w=[wsm16sb])
    P.op("pool", lambda: nc.gpsimd.tensor_copy(out=wsm16s[:, :, 64:128], in_=wsm[0:64, :, 0:64]),
         r=[wsmb], w=[wsm16sb])
    for half in range(2):
        pt, ptb = bank()
        ptv = pt[:].bitcast(BF16)
        for j in range(4):
            g = half * 4 + j
            P.op("pe", (lambda g=g, j=j, ptv=ptv: nc.tensor.transpose(
                out=ptv[:, j * 128:(j + 1) * 128], in_=wsm16[:, g, :], identity=ident[:])),
                r=[wsm16b, identb], w=[ptb])
        P.op("act", (lambda half=half, ptv=ptv: nc.scalar.copy(
            out=wsT[:, half * 4:(half + 1) * 4, :],
            in_=ptv[:, 0:512].rearrange("p (j t) -> p j t", t=128))), r=[ptb], w=[wsTb])
    pt, ptb = bank()
    ptv = pt[:].bitcast(BF16)
    for g in range(NH):
        P.op("pe", (lambda g=g, ptv=ptv: nc.tensor.transpose(
            out=ptv[:, g * 64:(g + 1) * 64], in_=wsm16s[:, g, :], identity=ident[0:64, 0:64])),
            r=[wsm16sb, identb], w=[ptb])
    P.op("act", (lambda ptv=ptv: nc.scalar.copy(
        out=wsT2[64:128, :, :], in_=ptv[64:128, 0:512].rearrange("p (g t) -> p g t", t=64))),
        r=[ptb], w=[wsT2b])
    P.op("pool", lambda: nc.gpsimd.memset(wsTS[:], 0.0), w=[wsTSb])
    P.op("pool", lambda: nc.gpsimd.tensor_copy(out=wsTS[0:64, :, 0:64], in_=wsT[0:64, :, 0:64]),
         r=[wsTb], w=[wsTSb])
    P.op("pool", lambda: nc.gpsimd.tensor_copy(out=wsTS[64:128, :, 64:128], in_=wsT2[64:128, :, :]),
         r=[wsT2b], w=[wsTSb])

    S2ORDER = [8, 6, 9, 7, 0, 1, 2, 3, 4, 5]

    def blocks_of_pass():
        L = []
        for nb in S2ORDER:
            L.append(("in", nb))
        for nb in range(4):
            L.append(("out", nb))
        for j in range(22):
            L.append(("up", j))
        for ob in range(4):
            for kg in range(3):
                L.append(("dn", ob, kg))
        return L

    PB = blocks_of_pass()
    NCONV = max(1, min(NCONV_MAX, NPASS - 1))
    assert len(PB) == NBLK
    wstate = {"next": 0}
    TOTAL = NPASS * NBLK

    def emit_load(gi):
        p, bi = divmod(gi, NBLK)
        blk = PB[bi]
        s = gi % NSLOT
        t, b = wslot[s], wslotb[s]
        cp = bi % NCONV
        if p <= cp:
            q = "pool"
            if blk[0] == "in":
                nb = blk[1]
                P.dma(q, t[:], w_in[:, nb * 512:(nb + 1) * 512].rearrange("(kc p) n -> p kc n", p=128), w=[b])
            elif blk[0] == "out":
                nb = blk[1]
                P.dma(q, t[:], w_out[:, nb * 512:(nb + 1) * 512].rearrange("(kc p) n -> p kc n", p=128), w=[b])
            elif blk[0] == "up":
                j = blk[1]
                nch = 2 if j < 21 else 1
                wdt = nch * 128
                P.dma(q, t[:, :, 0:wdt],
                      w_up[:, j * 256:j * 256 + wdt].rearrange("(kc p) n -> p kc n", p=128), w=[b])
                P.dma(q, t[:, :, 256:256 + wdt],
                      w_up[:, DFF + j * 256:DFF + j * 256 + wdt].rearrange("(kc p) n -> p kc n", p=128), w=[b])
            else:
                ob, kg = blk[1], blk[2]
                nfl = 16 if kg < 2 else 11
                P.dma(q, t[:, 0:nfl, :],
                      w_down[kg * 2048:kg * 2048 + nfl * 128, ob * 512:(ob + 1) * 512].rearrange(
                          "(fl p) n -> p fl n", p=128), w=[b])
            if p == cp and p < NPASS - 1:
                P.dma("sp", scr[bi], t[:].rearrange("p k n -> p (k n)"), r=[b], w=[scrb[bi]], key=b)
        else:
            P.dma("sp", t[:].rearrange("p k n -> p (k n)"), scr[bi], r=[scrb[bi]], w=[b], key=b)

    def wget(p, bi):
        gi = p * NBLK + bi
        while wstate["next"] <= min(gi + NSLOT - 1, TOTAL - 1):
            emit_load(wstate["next"])
            wstate["next"] += 1
        s = gi % NSLOT
        return wslot[s], wslotb[s]

    def rms_rstd(ss_ap, ssb, n_inv, width):
        rs, rsb = rs1R.next()
        P.op("dve", lambda: nc.vector.tensor_scalar(out=rs[:, 0:width], in0=ss_ap, scalar1=n_inv, scalar2=EPS,
                                                    op0=ALU.mult, op1=ALU.add), r=[ssb], w=[rsb])
        P.op("pool", lambda: nc.gpsimd.tensor_tensor(out=rs[:, 0:width], in0=rs[:, 0:width],
                                                     in1=mhalf[:, 0:width], op=ALU.pow),
             r=[rsb, mhalfb], w=[rsb])
        return rs, rsb

    def norm_chain(xt, xb):
        ss, ssb = ss1R.next()
        nb_, nbb = nbR.next()
        P.op("act", lambda: nc.scalar.activation(out=nb_[:], in_=xt[:], func=AF.Square, accum_out=ss[:, 0:1]),
             r=[xb], w=[nbb, ssb])
        rs, rsb = rms_rstd(ss[:, 0:1], ssb, 1.0 / D, 1)
        P.op("dve", lambda: nc.vector.tensor_scalar(out=nb_[:], in0=xt[:], scalar1=rs[:, 0:1], scalar2=None,
                                                    op0=ALU.mult), r=[xb, rsb], w=[nbb])
        return nb_, nbb

    def norm_chain_pre(xt, xb, ssp, sspb):
        rs0, rs0b = rs1R.next()
        P.op("dve", lambda: nc.vector.tensor_reduce(out=rs0[:, 0:1], in_=ssp[:, 0:4], axis=mybir.AxisListType.X,
                                                    op=ALU.add), r=[sspb], w=[rs0b])
        rs, rsb = rms_rstd(rs0[:, 0:1], rs0b, 1.0 / D, 1)
        nb_, nbb = nbR.next()
        P.op("dve", lambda: nc.vector.tensor_scalar(out=nb_[:, 0:D // 2], in0=xt[:, 0:D // 2], scalar1=rs[:, 0:1],
                                                    scalar2=None, op0=ALU.mult), r=[xb, rsb], w=[nbb])
        P.op("act", lambda: nc.scalar.activation(out=nb_[:, D // 2:D], in_=xt[:, D // 2:D], func=AF.Copy,
                                                 scale=rs[:, 0:1]), r=[xb, rsb], w=[nbb])
        return nb_, nbb

    def norm_T(nbp, t, gcol, gb):
        nb_, nbb = nbp
        for hf in range(2):
            pt, ptb = bank()
            ptv = pt[:].bitcast(BF16)
            for j in range(8):
                kc = hf * 8 + j
                P.op("pe", (lambda kc=kc, j=j, ptv=ptv: nc.tensor.transpose(
                    out=ptv[:, j * 128:(j + 1) * 128], in_=nb_[:, kc * 128:(kc + 1) * 128], identity=ident[:])),
                    r=[nbb, identb], w=[ptb])
            for j in range(8):
                kc = hf * 8 + j
                if hf == 0:
                    P.op("act", (lambda kc=kc, j=j, ptv=ptv: nc.scalar.activation(
                        out=nmix[:, kc, t * 128:(t + 1) * 128], in_=ptv[:, j * 128:(j + 1) * 128],
                        func=AF.Copy, scale=gcol[:, kc:kc + 1])), r=[ptb, gb], w=[nmixb[t][hf]])
                else:
                    P.op("dve", (lambda kc=kc, j=j, ptv=ptv: nc.vector.tensor_scalar(
                        out=nmix[:, kc, t * 128:(t + 1) * 128], in0=ptv[:, j * 128:(j + 1) * 128],
                        scalar1=gcol[:, kc:kc + 1], scalar2=None, op0=ALU.mult)), r=[ptb, gb], w=[nmixb[t][hf]])

    def norm_to_T(xt, xb, t, gcol, gb):
        norm_T(norm_chain(xt, xb), t, gcol, gb)

    def sample_attention(i, oa, oab):
        for s in range(2):
            for t in range(4):
                sl = s * 4 + t
                ckt, cktb = cstR.next()
                P.dma("sp", ckt[:], ck[s, t * 128:(t + 1) * 128, :], w=[cktb])
                for half in range(2):
                    pt, ptb = bank()
                    for j in range(4):
                        h = half * 4 + j
                        P.op("pe", (lambda pt=pt, j=j, h=h, ckt=ckt: nc.tensor.transpose(
                            out=pt[:, j * 128:(j + 1) * 128], in_=ckt[:, h * 128:(h + 1) * 128],
                            identity=identf[:])), r=[cktb, identfb], w=[ptb])
                    P.op("act", (lambda pt=pt, half=half, sl=sl: nc.scalar.copy(
                        out=kT[:, half * 4:(half + 1) * 4, sl * 128:(sl + 1) * 128],
                        in_=pt[:].rearrange("p (j t) -> p j t", t=128))), r=[ptb], w=[kTb[sl]])
                cvt, cvtb = cstR.next()
                P.dma("sp", cvt[:], cv[s, t * 128:(t + 1) * 128, :], w=[cvtb])
                P.op("dve", (lambda cvt=cvt, sl=sl: nc.vector.tensor_copy(
                    out=Va[:, sl, :, 0:128], in_=cvt[:].rearrange("p (h c) -> p h c", c=128))),
                    r=[cvtb], w=[Vab[sl]])
        stA, stB = [], []
        hstate = {}
        for h in range(NH):
            for s in range(2):
                ust = {}

                def A(h=h, s=s, ust=ust):
                    p0 = s * 64
                    q0 = i * 128 + s * 64
                    psA, psAb = bank()
                    psB, psBb = bank()
                    for t in range(4):
                        sl = s * 4 + t
                        if t == 3:
                            dps, dpb, c0 = psB, psBb, 64
                        else:
                            dps, dpb, c0 = psA, psAb, (2 - t) * 64
                        P.op("pe", (lambda dps=dps, c0=c0, sl=sl: nc.tensor.matmul(
                            dps[:, c0:c0 + 64], lhsT=kT[:, h, sl * 128:(sl + 1) * 128], rhs=qT[:, h, q0:q0 + 64],
                            start=True, stop=True)), r=[kTb[sl], qTb[i]], w=[dpb])
                    P.op("pe", (lambda: nc.tensor.matmul(
                        psB[p0:p0 + 64, 0:64], lhsT=kT[:, h, 8 * 128 + p0:8 * 128 + p0 + 64],
                        rhs=qT[:, h, q0:q0 + 64], start=True, stop=True)), r=[kTb[8], qTb[i]], w=[psBb])
                    stmp, stb = stmpR.next()
                    PT, PTb = PTR.next()
                    ust["PT"] = (PT, PTb)
                    P.op("dve", (lambda: nc.vector.scalar_tensor_tensor(
                        out=stmp[:, 64:128], in0=psB[:, 64:128], scalar=SCALE, in1=BT[:, h, 128:192],
                        op0=ALU.mult, op1=ALU.add)), r=[psBb, BTb], w=[stb])
                    P.op("dve", (lambda: nc.vector.scalar_tensor_tensor(
                        out=stmp[p0:p0 + 64, 0:64], in0=psB[p0:p0 + 64, 0:64], scalar=SCALE,
                        in1=BT[p0:p0 + 64, h, p0:p0 + 64], op0=ALU.mult, op1=ALU.add)), r=[psBb, BTb], w=[stb])
                    P.op("act", (lambda: nc.scalar.activation(
                        out=PT[:, 64:128], in_=stmp[:, 64:128], func=AF.Exp)), r=[stb], w=[PTb])
                    P.op("act", (lambda: nc.scalar.activation(
                        out=PT[p0:p0 + 64, 0:64], in_=stmp[p0:p0 + 64, 0:64], func=AF.Exp)), r=[stb], w=[PTb])
                    P.op("act", (lambda: nc.scalar.activation(
                        out=PT[64 - p0:128 - p0, 0:64], in_=BT[64 - p0:128 - p0, h, 64:128], func=AF.Copy,
                        scale=0.0)), r=[BTb], w=[PTb])
                    P.op("act", (lambda: nc.scalar.activation(
                        out=PT[:, 256:448], in_=psA[:, 0:192], func=AF.Exp, bias=chT[:, h:h + 1], scale=SCALE)),
                        r=[psAb, chb], w=[PTb])

                def B(h=h, s=s, ust=ust):
                    p0 = s * 64
                    PT, PTb = ust["PT"]
                    if s == 0:
                        hstate[h] = bank()
                    psO, psOb = hstate[h]
                    order = [(3, 64), (2, 256), (1, 320), (0, 384)]
                    for n_, (t, c0) in enumerate(order):
                        sl = s * 4 + t
                        P.op("pe", (lambda c0=c0, sl=sl, n_=n_: nc.tensor.matmul(
                            psO[p0:p0 + 64, 0:129], lhsT=PT[:, c0:c0 + 64], rhs=Va[:, sl, h, 0:129],
                            start=(n_ == 0), stop=False)), r=[PTb, Vab[sl]], w=[psOb])
                    P.op("pe", (lambda: nc.tensor.matmul(
                        psO[p0:p0 + 64, 0:129], lhsT=PT[:, 0:64], rhs=Va[:, 8, h, 0:129],
                        start=False, stop=True)), r=[PTb, Vab[8]], w=[psOb])
                    if s == 1:
                        rec, recb = recR.next()
                        P.op("dve", (lambda: nc.vector.reciprocal(
                            out=rec[:, 0:1], in_=psO[:, 128:129])), r=[psOb], w=[recb])
                        P.op("dve", (lambda: nc.vector.tensor_scalar(
                            out=oa[:, h * 128:(h + 1) * 128], in0=psO[:, 0:128], scalar1=rec[:, 0:1], scalar2=None,
                            op0=ALU.mult)), r=[psOb, recb], w=[oab])

                stA.append(A)
                stB.append((B, i if (h == NH - 1 and s == 1) else None))
        return stA, stB

    for p in range(NPASS if STOP > 0 else 0):
        last = (p == NPASS - 1)
        tiles = []
        for i in range(NT):
            g = p * NT + i
            tiles.append(("p", g) if g < NP else ("s",))
        xt_l = []
        for i, td in enumerate(tiles):
            xt, xb = xh.next()
            src = xp[td[1] * 128:(td[1] + 1) * 128, :] if td[0] == "p" else xs
            P.dma("sp", xt[:], src, w=[xb])
            xt_l.append((xt, xb))
        if p == 0 or not PREFETCH:
            for i in range(NT):
                norm_to_T(xt_l[i][0], xt_l[i][1], i, g1col, g1b)
        nall = [nmixb[t][hf] for t in range(NT) for hf in range(2)]

        if p >= STOPP and STOP < 2:
            break
        deferred = []

        def tick(flush=False):
            keep_ = []
            for ent in deferred:
                ent[0] -= 1
                if ent[0] <= 0 or flush:
                    ent[1]()
                else:
                    keep_.append(ent)
            deferred[:] = keep_

        for bi_, nb in enumerate(S2ORDER[:NBMAX]):
            wt, wb = wget(p, bi_)
            if nb in (6, 7):
                for c in range(4):
                    ps, psb = bank()
                    for kc in range(KC):
                        P.op("pe", (lambda ps=ps, wt=wt, kc=kc, c=c: nc.tensor.matmul(
                            ps[:, 0:T], lhsT=wt[:, kc, c * 128:(c + 1) * 128], rhs=nmix[:, kc, :],
                            start=(kc == 0), stop=(kc == KC - 1))), r=[wb] + nall, w=[psb])
                    cc = (nb - 6) * 4 + c
                    P.op("act", (lambda ps=ps, cc=cc: nc.scalar.activation(
                        out=uT[:, cc, :], in_=ps[:, 0:T], func=AF.Gelu)), r=[psb], w=[uTb])
                    tick()
                continue
            for i, td in enumerate(tiles):
                ps, psb = bank()
                for kc in range(KC):
                    P.op("pe", (lambda ps=ps, wt=wt, kc=kc, i=i: nc.tensor.matmul(
                        ps[:], lhsT=nmix[:, kc, i * 128:(i + 1) * 128], rhs=wt[:, kc, :],
                        start=(kc == 0), stop=(kc == KC - 1))), r=[wb, nmixb[i][0], nmixb[i][1]], w=[psb])
                tick()
                is_s = td[0] == "s"
                gslot = (td[1] % 8) if not is_s else None
                keep = is_s or td[1] >= KEEP0
                if nb < 4:
                    isq = nb < 2
                    h0 = (nb % 2) * 4
                    ss, ssb = ss1R.next()
                    qn, qnb = qnR.next()
                    for j in range(4):
                        P.op("act", (lambda ps=ps, j=j, ss=ss, qn=qn: nc.scalar.activation(
                            out=qn[:, j * 128:(j + 1) * 128], in_=ps[:, j * 128:(j + 1) * 128], func=AF.Square,
                            accum_out=ss[:, j:j + 1])), r=[psb], w=[qnb, ssb])
                    rs, rsb = rms_rstd(ss[:, 0:4], ssb, 1.0 / 128, 4)
                    gt, gtb = (gqb, gqbb) if isq else (gkb, gkbb)
                    for j in range(4):
                        P.op("dve", (lambda ps=ps, j=j, rs=rs, qn=qn, gt=gt: nc.vector.scalar_tensor_tensor(
                            out=qn[:, j * 128:(j + 1) * 128], in0=ps[:, j * 128:(j + 1) * 128],
                            scalar=rs[:, j:j + 1], in1=gt[:], op0=ALU.mult, op1=ALU.mult)),
                            r=[psb, rsb, gtb], w=[qnb])
                    if (not isq) and keep:
                        s32, s32b = st32R.next()
                        for j in range(4):
                            P.op("dve", (lambda ps=ps, j=j, rs=rs, s32=s32, gt=gt: nc.vector.scalar_tensor_tensor(
                                out=s32[:, j * 128:(j + 1) * 128], in0=ps[:, j * 128:(j + 1) * 128],
                                scalar=rs[:, j:j + 1], in1=gt[:], op0=ALU.mult, op1=ALU.mult)),
                                r=[psb, rsb, gtb], w=[s32b])
                        if is_s:
                            dst = nks[:, h0 * 128:(h0 + 4) * 128]
                        else:
                            r0 = (td[1] - KEEP0) * 128
                            dst = nkp[r0:r0 + 128, h0 * 128:(h0 + 4) * 128]
                        P.dma("sp", dst, s32[:], r=[s32b])
                    def qk_T(qn=qn, qnb=qnb, isq=isq, h0=h0, i=i, is_s=is_s, gslot=gslot):
                        pt, ptb = bank()
                        ptv = pt[:].bitcast(BF16)
                        for j in range(4):
                            P.op("pe", (lambda j=j: nc.tensor.transpose(
                                out=ptv[:, j * 128:(j + 1) * 128], in_=qn[:, j * 128:(j + 1) * 128],
                                identity=ident[:])), r=[qnb, identb], w=[ptb])
                        if isq:
                            P.op("act", (lambda: nc.scalar.copy(
                                out=qT[:, h0:h0 + 4, i * 128:(i + 1) * 128],
                                in_=ptv[:, 0:512].rearrange("p (j t) -> p j t", t=128))), r=[ptb], w=[qTb[i]])
                        else:
                            ks = gslot if not is_s else 8
                            P.op("act", (lambda: nc.scalar.copy(
                                out=kT[:, h0:h0 + 4, ks * 128:(ks + 1) * 128],
                                in_=ptv[:, 0:512].rearrange("p (j t) -> p j t", t=128))), r=[ptb],
                                w=[kTb[ks]])
                    deferred.append([2, qk_T])
                elif nb < 6:
                    h0 = (nb - 4) * 4
                    vs_ = gslot if not is_s else 8
                    vbuf = Vab[vs_]
                    if VMODE & 1:
                        P.op("act", (lambda ps=ps, h0=h0, vs_=vs_: nc.scalar.copy(
                            out=Va[:, vs_, h0:h0 + 4, 0:128],
                            in_=ps[:].rearrange("p (j t) -> p j t", t=128))), r=[psb], w=[vbuf])
                    if keep and (VMODE & 2):
                        s32, s32b = st32R.next()
                        P.op("dve", (lambda ps=ps, s32=s32: nc.vector.tensor_copy(out=s32[:], in_=ps[:])),
                             r=[psb], w=[s32b])
                        if is_s:
                            dst = nvs[:, h0 * 128:(h0 + 4) * 128]
                        else:
                            r0 = (td[1] - KEEP0) * 128
                            dst = nvp[r0:r0 + 128, h0 * 128:(h0 + 4) * 128]
                        P.dma("sp", dst, s32[:], r=[s32b])
                else:
                    g0 = (nb - 8) * 4
                    gv, gvb = gvR.next()
                    P.op("act", (lambda ps=ps, gv=gv: nc.scalar.activation(out=gv[:], in_=ps[:], func=AF.Gelu)),
                         r=[psb], w=[gvb])
                    bs_, bsb_ = bnsR.next()
                    bm_, bmb_ = bnmR.next()
                    for j in range(4):
                        P.op("dve", (lambda j=j, gv=gv, bs_=bs_: nc.vector.bn_stats(
                            out=bs_[:, j, :], in_=gv[:, j * 128:(j + 1) * 128])), r=[gvb], w=[bsb_])
                    for j in range(4):
                        P.op("dve", (lambda j=j, bs_=bs_, bm_=bm_: nc.vector.bn_aggr(
                            out=bm_[:, j, :], in_=bs_[:, j, :])), r=[bsb_], w=[bmb_])
                    rs, rsb = rs1R.next()
                    P.op("dve", (lambda rs=rs, bm_=bm_: nc.vector.tensor_scalar(
                        out=rs[:, 0:4], in0=bm_[:, :, 1], scalar1=EPS, scalar2=None, op0=ALU.add)),
                        r=[bmb_], w=[rsb])
                    P.op("pool", (lambda rs=rs: nc.gpsimd.tensor_tensor(
                        out=rs[:, 0:4], in0=rs[:, 0:4], in1=mhalf[:, 0:4], op=ALU.pow)),
                        r=[rsb, mhalfb], w=[rsb])
                    for j in range(4):
                        P.op("dve", (lambda j=j, gv=gv, bm_=bm_, rs=rs: nc.vector.tensor_scalar(
                            out=gv[:, j * 128:(j + 1) * 128], in0=gv[:, j * 128:(j + 1) * 128],
                            scalar1=bm_[:, j, 0:1], scalar2=rs[:, j:j + 1], op0=ALU.subtract, op1=ALU.mult)),
                            r=[gvb, bmb_, rsb], w=[gvb])
                    gv3 = gv[:].rearrange("p (j c) -> p j c", c=128)
                    P.op("pool", (lambda gv3=gv3: nc.gpsimd.tensor_tensor(
                        out=gv3, in0=gv3, in1=lngb[:].unsqueeze(1).to_broadcast([128, 4, 128]), op=ALU.mult)),
                        r=[gvb, lngbb], w=[gvb])
                    if is_s:
                        P.op("pool", (lambda gv3=gv3: nc.gpsimd.tensor_tensor(
                            out=gv3, in0=gv3, in1=lnbb[:].unsqueeze(1).to_broadcast([128, 4, 128]), op=ALU.add)),
                            r=[gvb, lnbbb], w=[gvb])
                        P.op("pool", (lambda gv=gv, i=i, g0=g0: nc.gpsimd.tensor_copy(
                            out=vn[:, i, g0 * 128:(g0 + 4) * 128], in_=gv[:])), r=[gvb], w=[vnb[i]])
                        P.dma("sp", ngs[:, g0 * 128:(g0 + 4) * 128], gv[:], r=[gvb])
                    else:
                        P.op("pool", (lambda gv3=gv3, i=i, g0=g0: nc.gpsimd.tensor_tensor(
                            out=vn[:, i, g0 * 128:(g0 + 4) * 128].rearrange("p (j c) -> p j c", c=128), in0=gv3,
                            in1=lnbb[:].unsqueeze(1).to_broadcast([128, 4, 128]), op=ALU.add)),
                            r=[gvb, lnbbb], w=[vnb[i]])

        tick(flush=True)
        if p >= STOPP and STOP < 3:
            break
        stageA, stageB = [], []
        oas = {}
        for i, td in enumerate(tiles):
            if td[0] != "p":
                continue
            oa, oab = oaR.next()
            oas[i] = (oa, oab)
            g = td[1]
            navail = min(5, g + 1)
            tstate = {"psO": None}
            for h in range(NH):
                ust = {}

                def A(i=i, g=g, h=h, navail=navail, ust=ust):
                    psA, psAb = bank()
                    psB, psBb = bank()
                    for tr in range(4, 4 - navail, -1):
                        gk = g - 4 + tr
                        ksl = gk % 8
                        if tr >= 3:
                            dstps, dstb, c0 = psB, psBb, (4 - tr) * 128
                        else:
                            dstps, dstb, c0 = psA, psAb, (2 - tr) * 128
                        P.op("pe", (lambda dstps=dstps, c0=c0, ksl=ksl, tr=tr: nc.tensor.matmul(
                            dstps[:, c0:c0 + 128], lhsT=kT[:, h, ksl * 128:(ksl + 1) * 128],
                            rhs=qT[:, h, i * 128:(i + 1) * 128], start=True, stop=(tr != 0))),
                            r=[kTb[ksl], qTb[i]], w=[dstb])
                        if tr == 0:
                            P.op("pe", (lambda dstps=dstps, c0=c0: nc.tensor.matmul(
                                dstps[:, c0:c0 + 128], lhsT=mku[:, :], rhs=mkv[:, :], start=False, stop=True)),
                                r=[mkb], w=[dstb])
                    nB = min(navail, 2)
                    nA = navail - nB
                    stmp, stb = stmpR.next()
                    PT, PTb = PTR.next()
                    ust["PT"] = (PT, PTb)
                    P.op("dve", (lambda: nc.vector.scalar_tensor_tensor(
                        out=stmp[:, 0:nB * 128], in0=psB[:, 0:nB * 128], scalar=SCALE, in1=BT[:, h, 0:nB * 128],
                        op0=ALU.mult, op1=ALU.add)), r=[psBb, BTb], w=[stb])
                    P.op("act", (lambda: nc.scalar.activation(
                        out=PT[:, 0:nB * 128], in_=stmp[:, 0:nB * 128], func=AF.Exp)), r=[stb], w=[PTb])
                    if nA > 0:
                        P.op("act", (lambda: nc.scalar.activation(
                            out=PT[:, 256:256 + nA * 128], in_=psA[:, 0:nA * 128], func=AF.Exp,
                            bias=chT[:, h:h + 1], scale=SCALE)), r=[psAb, chb], w=[PTb])

                def B(i=i, g=g, h=h, navail=navail, ust=ust, tstate=tstate, oa=oa, oab=oab):
                    PT, PTb = ust["PT"]
                    if h % 3 == 0:
                        tstate["psO"] = bank()
                        tstate["h0"] = h
                    psO, psOb = tstate["psO"]
                    jo = (h % 3) * 160
                    cnt = 0
                    for tr in range(4, 4 - navail, -1):
                        gk = g - 4 + tr
                        ksl = gk % 8
                        c0 = (4 - tr) * 128 if tr >= 3 else 256 + (2 - tr) * 128
                        P.op("pe", (lambda c0=c0, ksl=ksl, cnt=cnt: nc.tensor.matmul(
                            psO[:, jo:jo + 129], lhsT=PT[:, c0:c0 + 128], rhs=Va[:, ksl, h, 0:129],
                            start=(cnt == 0), stop=(cnt == navail - 1))), r=[PTb, Vab[ksl]], w=[psOb])
                        cnt += 1
                    if h % 3 == 2 or h == NH - 1:
                        hb0 = tstate["h0"]
                        nh_ = h - hb0 + 1
                        rec, recb = recR.next()
                        P.op("dve", (lambda: nc.vector.reciprocal(
                            out=rec[:, 0:nh_],
                            in_=psO[:, 0:160 * nh_].rearrange("p (j c) -> p j c", c=160)[:, :, 128])),
                            r=[psOb], w=[recb])
                        P.op("dve", (lambda: nc.vector.tensor_tensor(
                            out=oa[:, hb0 * 128:(hb0 + nh_) * 128].rearrange("p (j c) -> p j c", c=128),
                            in0=psO[:, 0:160 * nh_].rearrange("p (j c) -> p j c", c=160)[:, :, 0:128],
                            in1=rec[:, 0:nh_].unsqueeze(2).to_broadcast([128, nh_, 128]), op=ALU.mult)),
                            r=[psOb, recb], w=[oab])

                stageA.append(A)
                stageB.append((B, i if h == NH - 1 else None))

        def oa_transposes(i):
            oa, oab = oas[i]
            pt, ptb = bank()
            ptv = pt[:].bitcast(BF16)
            for h in range(NH):
                P.op("pe", (lambda h=h: nc.tensor.transpose(
                    out=ptv[:, h * 128:(h + 1) * 128], in_=oa[:, h * 128:(h + 1) * 128], identity=ident[:])),
                    r=[oab, identb], w=[ptb])
            P.op("act", (lambda: nc.scalar.copy(
                out=nmix[:, 0:8, i * 128:(i + 1) * 128],
                in_=ptv[:, 0:1024].rearrange("p (j t) -> p j t", t=128))), r=[ptb], w=[nmixb[i][0]])

        def run_pipeline(stA_, stB_, LAG=2):
            pend_T = []
            nU = len(stA_)
            for k in range(nU + LAG):
                if k < nU:
                    stA_[k]()
                if k >= LAG:
                    Bf, tdone = stB_[k - LAG]
                    Bf()
                    if tdone is not None:
                        pend_T.append((k + 2, tdone))
                while pend_T and pend_T[0][0] <= k:
                    oa_transposes(pend_T.pop(0)[1])
            for _, ti in pend_T:
                oa_transposes(ti)

        run_pipeline(stageA, stageB)
        for i, td in enumerate(tiles):
            if td[0] == "s":
                oa, oab = oaR.next()
                oas[i] = (oa, oab)
                sA, sB = sample_attention(i, oa, oab)
                run_pipeline(sA, sB)
        if p >= STOPP and STOP < 4:
            break
        for i, td in enumerate(tiles):
            gt_, gtb_ = gtmpR.next()
            for half in range(2):
                ps, psb = bank()
                for j in range(4):
                    g = half * 4 + j
                    if td[0] == "p":
                        P.op("pe", (lambda ps=ps, j=j, g=g, i=i: nc.tensor.matmul(
                            ps[:, j * 128:(j + 1) * 128], lhsT=vn[:, i, g * 128:(g + 1) * 128], rhs=wsT[:, g, :],
                            start=True, stop=True)), r=[vnb[i], wsTb], w=[psb])
                    else:
                        P.op("pe", (lambda ps=ps, j=j, g=g, i=i: nc.tensor.matmul(
                            ps[:, j * 128:(j + 1) * 128], lhsT=vn[:, i, g * 128:(g + 1) * 128], rhs=wsTS[:, g, :],
                            start=True, stop=True)), r=[vnb[i], wsTSb], w=[psb])
                if td[0] == "p":
                    P.op("dve", (lambda ps=ps, gt_=gt_, half=half: nc.vector.tensor_tensor(
                        out=gt_[:, half * 512:(half + 1) * 512], in0=ps[:],
                        in1=bsb[:, half * 4:(half + 1) * 4, :].rearrange("p g t -> p (g t)"), op=ALU.add)),
                        r=[psb, bsbb], w=[gtb_])
                else:
                    for sq in range(2):
                        P.op("dve", (lambda ps=ps, gt_=gt_, half=half, sq=sq: nc.vector.tensor_tensor(
                            out=gt_[:, half * 512:(half + 1) * 512].rearrange(
                                "p (g t) -> p g t", t=128)[:, :, sq * 64:(sq + 1) * 64],
                            in0=ps[:].rearrange("p (g t) -> p g t", t=128)[:, :, sq * 64:(sq + 1) * 64],
                            in1=bsb[:, half * 4:(half + 1) * 4, 0:64], op=ALU.add)),
                            r=[psb, bsbb], w=[gtb_])
            P.op("pool", (lambda gt_=gt_, i=i: nc.gpsimd.tensor_tensor(
                out=nmix[:, 8:16, i * 128:(i + 1) * 128],
                in0=gt_[:].rearrange("p (g t) -> p g t", t=128),
                in1=uT[:, :, i * 128:(i + 1) * 128], op=ALU.mult)), r=[gtb_, uTb], w=[nmixb[i][1]])

        if p >= STOPP and STOP < 5:
            break
        ssps = [ss1R.next() for _ in range(NT)]
        for nb in range(4):
            wt, wb = wget(p, 10 + nb)
            for i in range(NT):
                ps, psb = bank()
                xt, xb = xt_l[i]
                for kc in range(KC):
                    P.op("pe", (lambda ps=ps, wt=wt, kc=kc, i=i: nc.tensor.matmul(
                        ps[:], lhsT=nmix[:, kc, i * 128:(i + 1) * 128], rhs=wt[:, kc, :],
                        start=(kc == 0), stop=(kc == KC - 1))), r=[wb, nmixb[i][0], nmixb[i][1]], w=[psb])
                P.op("dve", (lambda ps=ps, xt=xt, nb=nb: nc.vector.tensor_tensor(
                    out=xt[:, nb * 512:(nb + 1) * 512], in0=ps[:], in1=xt[:, nb * 512:(nb + 1) * 512],
                    op=ALU.add)), r=[psb, xb], w=[xb])
                jq, jqb = qnR.next()
                P.op("act", (lambda xt=xt, nb=nb, jq=jq, i=i: nc.scalar.activation(
                    out=jq[:], in_=xt[:, nb * 512:(nb + 1) * 512], func=AF.Square,
                    accum_out=ssps[i][0][:, nb:nb + 1])), r=[xb], w=[jqb, ssps[i][1]])
                if nb == 3:
                    if i >= 1:
                        norm_T(n2pend, i - 1, g2col, g2b)
                    n2pend = norm_chain_pre(xt, xb, ssps[i][0], ssps[i][1])
        norm_T(n2pend, NT - 1, g2col, g2b)
        if p >= STOPP and STOP < 6:
            break
        nall = [nmixb[t][hf] for t in range(NT) for hf in range(2)]

        if p >= STOPP and STOP < 7:
            break
        segs = []
        c = 0
        npr = sum(1 for td in tiles if td[0] == "p")
        if npr:
            segs.append((0, npr * 128, "prev" if p > 0 else None))
        if last:
            segs.append((npr * 128, npr * 128 + 64, "sA"))
            segs.append((npr * 128 + 64, npr * 128 + 128, "sB"))
        if last:
            P.dma("sp", haloS[:, 0], cst[0], w=[haloSb])
            P.dma("sp", haloS[:, 1], cst[1], w=[haloSb])
        for j in range(22):
            wt, wb = wget(p, 14 + j)
            nch = 2 if j < 21 else 1
            for cch in range(nch):
                f = 2 * j + cch
                psa, psab = bank()
                psg, psgb = bank()
                for kc in range(KC):
                    P.op("pe", (lambda psa=psa, wt=wt, kc=kc, cch=cch: nc.tensor.matmul(
                        psa[:, 0:T], lhsT=wt[:, kc, cch * 128:(cch + 1) * 128], rhs=nmix[:, kc, :],
                        start=(kc == 0), stop=(kc == KC - 1))), r=[wb] + nall, w=[psab])
                for kc in range(KC):
                    P.op("pe", (lambda psg=psg, wt=wt, kc=kc, cch=cch: nc.tensor.matmul(
                        psg[:, 0:T], lhsT=wt[:, kc, 256 + cch * 128:256 + (cch + 1) * 128], rhs=nmix[:, kc, :],
                        start=(kc == 0), stop=(kc == KC - 1))), r=[wb] + nall, w=[psgb])
                acc, accb = accR.next()
                P.op("act", (lambda psa=psa, acc=acc, f=f: nc.scalar.activation(
                    out=acc[:], in_=psa[:, 0:T], func=AF.Identity, scale=convw[:, f, 2:3],
                    bias=convb[:, f:f + 1])), r=[psab, convwb, convbb], w=[accb])
                for (c0, c1, hs) in segs:
                    P.op("dve", (lambda psa=psa, acc=acc, f=f, c0=c0, c1=c1: nc.vector.scalar_tensor_tensor(
                        out=acc[:, c0 + 1:c1], in0=psa[:, c0:c1 - 1], scalar=convw[:, f, 1:2],
                        in1=acc[:, c0 + 1:c1], op0=ALU.mult, op1=ALU.add)), r=[psab, convwb, accb], w=[accb])
                    P.op("dve", (lambda psa=psa, acc=acc, f=f, c0=c0, c1=c1: nc.vector.scalar_tensor_tensor(
                        out=acc[:, c0 + 2:c1], in0=psa[:, c0:c1 - 2], scalar=convw[:, f, 0:1],
                        in1=acc[:, c0 + 2:c1], op0=ALU.mult, op1=ALU.add)), r=[psab, convwb, accb], w=[accb])
                    if hs is not None:
                        if hs == "prev":
                            hap, hb_ = halo[:, f, :], halob
                        elif hs == "sA":
                            hap, hb_ = haloS[:, 0, f, :], haloSb
                        else:
                            hap, hb_ = haloS[:, 1, f, :], haloSb
                        P.op("dve", (lambda acc=acc, f=f, c0=c0, hap=hap: nc.vector.scalar_tensor_tensor(
                            out=acc[:, c0:c0 + 2], in0=hap, scalar=convw[:, f, 0:1],
                            in1=acc[:, c0:c0 + 2], op0=ALU.mult, op1=ALU.add)), r=[hb_, convwb, accb], w=[accb])
                        P.op("dve", (lambda acc=acc, f=f, c0=c0, hap=hap: nc.vector.scalar_tensor_tensor(
                            out=acc[:, c0:c0 + 1], in0=hap[:, 1:2], scalar=convw[:, f, 1:2],
                            in1=acc[:, c0:c0 + 1], op0=ALU.mult, op1=ALU.add)), r=[hb_, convwb, accb], w=[accb])
                if not last:
                    P.op("dve", (lambda psa=psa, f=f: nc.vector.tensor_copy(
                        out=halo[:, f, :], in_=psa[:, T - 2:T])), r=[psab], w=[halob])
                else:
                    e0 = npr * 128 - 2
                    P.op("dve", (lambda psa=psa, f=f, e0=e0: nc.vector.tensor_copy(
                        out=aconv[:, 0, :, f], in_=psa[:, e0:e0 + 2])), r=[psab], w=[aconvb])
                    P.op("dve", (lambda psa=psa, f=f, e0=e0: nc.vector.tensor_copy(
                        out=aconv[:, 1:3, :, f],
                        in_=psa[:, e0 + 64:e0 + 192].rearrange("p (s c) -> p s c", c=64)[:, :, 0:2])),
                        r=[psab], w=[aconvb])
                P.op("act", (lambda acc=acc: nc.scalar.activation(out=acc[:], in_=acc[:], func=AF.Silu)),
                     r=[accb], w=[accb])
                P.op("dve", (lambda acc=acc, psg=psg, f=f: nc.vector.tensor_tensor(
                    out=mT[:, f, :], in0=acc[:], in1=psg[:, 0:T], op=ALU.mult)), r=[accb, psgb], w=[mTb[f]])

        if p >= STOPP and STOP < 8:
            break
        for ob in range(4):
            pbs = [bank() for _ in range(NT)]
            for kg in range(3):
                wt, wb = wget(p, 36 + ob * 3 + kg)
                nfl = 16 if kg < 2 else 11
                for i in range(NT):
                    ps, psb = pbs[i]
                    for fl in range(nfl):
                        f = kg * 16 + fl
                        P.op("pe", (lambda ps=ps, wt=wt, fl=fl, f=f, i=i: nc.tensor.matmul(
                            ps[:], lhsT=mT[:, f, i * 128:(i + 1) * 128], rhs=wt[:, fl, :],
                            start=(f == 0), stop=(f == NF - 1))), r=[wb, mTb[f]], w=[psb])
                if kg == 0 and PREFETCH and not last:
                    if ob >= 1:
                        norm_T(n1pend, ob - 1, g1col, g1b)
                    if ob < NT:
                        gnext = (p + 1) * NT + ob
                        srcn = xp[gnext * 128:(gnext + 1) * 128, :] if gnext < NP else xs
                        P.dma("sp", xtmp, srcn, w=[xtmpb])
                        n1pend = norm_chain(xtmp, xtmpb)
            for i, td in enumerate(tiles):
                ps, psb = pbs[i]
                xt, xb = xt_l[i]
                P.op("dve", (lambda ps=ps, xt=xt, ob=ob: nc.vector.tensor_tensor(
                    out=xt[:, ob * 512:(ob + 1) * 512], in0=ps[:], in1=xt[:, ob * 512:(ob + 1) * 512],
                    op=ALU.add)), r=[psb, xb], w=[xb])
                dst = (yp[td[1] * 128:(td[1] + 1) * 128, ob * 512:(ob + 1) * 512] if td[0] == "p"
                       else ys[:, ob * 512:(ob + 1) * 512])
                P.dma("sp", dst, xt[:, ob * 512:(ob + 1) * 512], r=[xb], key=xb)

        if p >= STOPP and STOP < 9:
            break
        if last:
            for s3 in range(3):
                pt, ptb = bank()
                P.op("pe", (lambda pt=pt, s3=s3: nc.tensor.transpose(
                    out=pt[0:2 * NF, 0:128], in_=aconv[:, s3, :, :].rearrange("p r f -> p (r f)"),
                    identity=identf[:])), r=[aconvb, identfb], w=[ptb])
                aT, aTb = st32R.next()
                P.op("act", (lambda pt=pt, aT=aT: nc.scalar.copy(out=aT[0:2 * NF, 0:128], in_=pt[0:2 * NF, 0:128])),
                     r=[ptb], w=[aTb])
                for r_ in range(2):
                    dst = (ncp[r_] if s3 == 0 else ncs[s3 - 1, r_]).rearrange("(c p) -> c p", p=128)
                    P.dma("sp", dst, aT[r_ * NF:(r_ + 1) * NF, 0:128], r=[aTb])

    stats = P.finalize()
    return nc, es, stats


_CACHE = {}


def _get_prog(NP):
    if NP not in _CACHE:
        _CACHE[NP] = build(NP)
    return _CACHE[NP]


def make_in_maps(inp, n_cores):
    f = lambda a: np.ascontiguousarray(np.asarray(a, dtype=np.float32))
    x_prompt = f(inp["x_prompt"]); x_sample = f(inp["x_sample"])
    ck = f(inp["cache_attn_k"])[0]; cv = f(inp["cache_attn_v"])[0]; cst = f(inp["state_ffn_conv"])[0]
    table = f(inp["rel_bias_table"])[0]
    tpad = np.concatenate([table, np.repeat(table[:, 256:257], 127, axis=1)], axis=1)
    tpadR = f(np.repeat(tpad[:, None, :], 128, axis=1).reshape(NH, 128 * 384))
    shared = {
        "w_in": f(inp["w_in"])[0], "w_out": f(inp["w_out"])[0], "w_up": f(inp["w_up"])[0],
        "w_down": f(inp["w_down"])[0],
        "g1col": f(f(inp["norm_mix_g"])[0].reshape(KC, 128).T),
        "g2col": f(f(inp["norm_ffn_g"])[0].reshape(KC, 128).T),
        "gqb": f(np.broadcast_to(f(inp["q_norm_g"])[0][None, :], (128, 128))),
        "gkb": f(np.broadcast_to(f(inp["k_norm_g"])[0][None, :], (128, 128))),
        "tpadR": tpadR,
        "ch": f(np.broadcast_to(table[:, 256][None, :], (128, NH))),
        "lngb": f(np.broadcast_to(f(inp["gmlp_ln_g"])[0][None, :], (128, 128))),
        "lnbb": f(np.broadcast_to(f(inp["gmlp_ln_b"])[0][None, :], (128, 128))),
        "ws": f(f(inp["gmlp_w_s"])[0].transpose(1, 0, 2)),
        "bsb": f(np.broadcast_to(f(inp["gmlp_b_s"])[0][None, :, :], (128, NH, 128))),
        "convw": f(f(inp["ffn_conv_w"])[0].reshape(3, NF, 128).transpose(2, 1, 0)),
        "convb": f(f(inp["ffn_conv_b"])[0].reshape(NF, 128).T),
    }
    maps = []
    for c in range(n_cores):
        m = dict(shared)
        m["xp"] = x_prompt[c]
        m["xs"] = f(x_sample[2 * c:2 * c + 2].reshape(128, D))
        m["ck"] = f(ck[2 * c:2 * c + 2].reshape(2, 512, 1024))
        m["cv"] = f(cv[2 * c:2 * c + 2].reshape(2, 512, 1024))
        m["cst"] = f(cst[2 * c:2 * c + 2].reshape(2, 2, NF, 128).transpose(0, 3, 2, 1))
        maps.append(m)
    return maps


def assemble(results, n_cores, SEQ):
    B, DB = n_cores, 2 * n_cores
    y_p = np.empty((B, SEQ, D), np.float32)
    y_s = np.empty((DB, 64, D), np.float32)
    nk_p = np.empty((1, B, 512, NH, 128), np.float32)
    nv_p = np.empty((1, B, 512, NH, 128), np.float32)
    nc_p = np.empty((1, B, 2, DFF), np.float32)
    nk_s = np.empty((1, DB, 64, NH, 128), np.float32)
    nv_s = np.empty((1, DB, 64, NH, 128), np.float32)
    ng_s = np.empty((1, DB, 64, 1024), np.float32)
    nc_s = np.empty((1, DB, 2, DFF), np.float32)
    for c in range(n_cores):
        r = results[c]
        y_p[c] = r["yp"]
        y_s[2 * c:2 * c + 2] = r["ys"].reshape(2, 64, D)
        nk_p[0, c] = r["nkp"].reshape(512, NH, 128)
        nv_p[0, c] = r["nvp"].reshape(512, NH, 128)
        nc_p[0, c] = r["ncp"]
        nk_s[0, 2 * c:2 * c + 2] = r["nks"].reshape(2, 64, NH, 128)
        nv_s[0, 2 * c:2 * c + 2] = r["nvs"].reshape(2, 64, NH, 128)
        ng_s[0, 2 * c:2 * c + 2] = r["ngs"].reshape(2, 64, 1024)
        nc_s[0, 2 * c:2 * c + 2] = r["ncs"]
    return (y_p, y_s, nk_p, nv_p, nc_p, nk_s, nv_s, ng_s, nc_s)


def kernel(**inputs):
    SEQ = int(np.asarray(inputs["x_prompt"]).shape[1])
    n_cores = int(np.asarray(inputs["x_prompt"]).shape[0])
    nc, _es, _stats = _get_prog(SEQ // 128)
    maps = make_in_maps(inputs, n_cores)
    res = run_bass_kernel_spmd(nc, maps, core_ids=list(range(n_cores)))
    return assemble(res.results, n_cores, SEQ)
```

```python
import numpy as np
from contextlib import ExitStack
import concourse.bass as bass
import concourse.mybir as mybir
from concourse.bass_utils import run_bass_kernel_spmd

F32 = mybir.dt.float32
BF16 = mybir.dt.bfloat16
AF = mybir.ActivationFunctionType
ALU = mybir.AluOpType

D = 2048
KC = 16
DIN = 5120
DFF = 5504
NF = 43
NH = 8
NT = 3
T = NT * 128
SCALE = 128.0 ** -0.5
EPS = 1e-6
NEG = -30000.0
N_CORES = 8
import os
STOP = int(os.environ.get('MK_STOP', '99'))
NBMAX = int(os.environ.get('MK_NB', '10'))
STOPP = int(os.environ.get('MK_STOPP', '0'))
PREFETCH = int(os.environ.get('MK_PREFETCH', '1'))
NCONV_MAX = int(os.environ.get('MK_NCONV', '1'))
VMODE = int(os.environ.get('MK_VMODE', '3'))


class Buf:
    __slots__ = ("name", "w", "r", "rd", "al", "sem", "cnt", "excl")

    def __init__(self, name):
        self.name = name
        self.w = None
        self.r = {}
        self.rd = []
        self.al = []
        self.sem = None
        self.cnt = 0
        self.excl = False


class Instr:
    __slots__ = ("eng", "fn", "dma", "deps", "signal", "sem", "val", "key")

    def __init__(self, eng, fn, dma):
        self.eng = eng
        self.fn = fn
        self.dma = dma
        self.deps = []
        self.signal = False
        self.sem = None
        self.val = 0
        self.key = None


class Prog:
    ENG = ("pe", "act", "dve", "pool", "sp")

    def __init__(self, nc, es):
        self.nc = nc
        self.es = es
        self.e = {"pe": nc.tensor, "act": nc.scalar, "dve": nc.vector, "pool": nc.gpsimd, "sp": nc.sync}
        self.ins = []
        self.keybufs = []

    def _dep(self, I, reads, writes):
        cand = {}

        def add(Dp, raw):
            if Dp is None or Dp is I:
                return
            k = id(Dp)
            if k in cand:
                cand[k] = (Dp, cand[k][1] or raw)
            else:
                cand[k] = (Dp, raw)

        for b in reads:
            add(b.w, True)
            if b.excl:
                for rr in b.r.values():
                    if rr.eng != I.eng:
                        add(rr, False)
        for b in writes:
            for x in [b] + b.al:
                add(x.w, False)
                for rr in x.r.values():
                    add(rr, False)
                for rr in x.rd:
                    add(rr, False)
        for Dp, raw in cand.values():
            if (not I.dma) and (not Dp.dma) and Dp.eng == I.eng and I.eng == "pe":
                continue
            I.deps.append(Dp)
            Dp.signal = True
        for b in reads:
            if I.dma:
                b.rd.append(I)
            else:
                b.r[I.eng] = I
        for b in writes:
            for x in [b] + b.al:
                x.w = I
                x.r = {}
                x.rd = []

    def op(self, eng, fn, r=(), w=()):
        I = Instr(eng, fn, False)
        self._dep(I, r, w)
        self.ins.append(I)
        return I

    def dma(self, eng, out, in_, r=(), w=(), key=None):
        ee = self.e[eng]
        I = Instr(eng, lambda: ee.dma_start(out=out, in_=in_), True)
        I.key = key if key is not None else (w[0] if w else r[0])
        self._dep(I, r, w)
        self.ins.append(I)
        return I

    def finalize(self):
        nc, es = self.nc, self.es
        esem = {}
        for e in self.ENG:
            esem[e] = (es.enter_context(nc.semaphore("c_" + e)), "c_" + e)
        cnt = {e: 0 for e in self.ENG}
        nd = 0
        for I in self.ins:
            if I.dma:
                b = I.key
                kind = 1 if I.eng == "pool" else 0
                if b.sem is None:
                    b.sem = {}
                    b.cnt = {}
                if kind not in b.sem:
                    b.sem[kind] = (es.enter_context(nc.semaphore("d%d_%s" % (nd, b.name))), "d%d" % nd)
                    b.cnt[kind] = 0
                    nd += 1
                    self.keybufs.append((b, kind))
                b.cnt[kind] += 16
                I.sem = b.sem[kind]
                I.val = b.cnt[kind]
            elif I.signal:
                cnt[I.eng] += 1
                I.sem = esem[I.eng]
                I.val = cnt[I.eng]
        seen = {e: {} for e in self.ENG}
        nwait = 0
        for I in self.ins:
            ee = self.e[I.eng]
            need = {}
            for Dp in I.deps:
                k = Dp.sem[1]
                if k not in need or need[k][1] < Dp.val:
                    need[k] = (Dp.sem[0], Dp.val)
            sn = seen[I.eng]
            for k, (sem, val) in need.items():
                if sn.get(k, 0) < val:
                    ee.wait_ge(sem, val)
                    sn[k] = val
                    nwait += 1
            x = I.fn()
            if I.dma:
                x.then_inc(I.sem[0], 16)
            elif I.signal:
                x.then_inc(I.sem[0], 1)
        for b, kind in self.keybufs:
            nc.sync.wait_ge(b.sem[kind][0], b.cnt[kind])
        for e in ("pe", "act", "dve", "pool"):
            if cnt[e] > 0:
                nc.sync.wait_ge(esem[e][0], cnt[e])
        return len(self.ins), nwait, nd


class Ring:
    def __init__(self, nc, es, name, shape, dtype, n):
        self.t = [es.enter_context(nc.sbuf_tensor("r_%s%d" % (name, i), shape, dtype)) for i in range(n)]
        self.b = [Buf("%s%d" % (name, i)) for i in range(n)]
        self.i = 0

    def next(self):
        k = self.i % len(self.t)
        self.i += 1
        return self.t[k], self.b[k]

    @classmethod
    def views(cls, pairs):
        r = cls.__new__(cls)
        r.t = [a for a, _ in pairs]
        r.b = [b for _, b in pairs]
        r.i = 0
        return r


S2ORDER = [8, 6, 9, 7, 0, 1, 2, 3, 4, 5]


def blocks_of_pass():
    L = []
    for nb in S2ORDER:
        L.append(("in", nb))
    for nb in range(4):
        L.append(("out", nb))
    for j in range(22):
        L.append(("up", j))
    for ob in range(4):
        for kg in range(3):
            L.append(("dn", ob, kg))
    return L


def build(NP):
    assert (NP + 1) % NT == 0 and NP >= 4
    NPASS = (NP + 1) // NT
    SEQ = NP * 128
    KEEP0 = NP - 4

    nc = bass.Bass("TRN2", target_bir_lowering=False)

    def din(name, shape, dt=F32):
        return nc.dram_tensor(name, list(shape), dt, kind="ExternalInput").ap()

    def dout(name, shape):
        return nc.dram_tensor(name, list(shape), F32, kind="ExternalOutput").ap()

    xp = din("xp", [SEQ, D])
    xs = din("xs", [128, D])
    ck = din("ck", [2, 512, 1024])
    cv = din("cv", [2, 512, 1024])
    cst = din("cst", [2, 128, NF, 2])
    wblk = din("wblk", [48, 128, KC * 512])
    g1col_d = din("g1col", [128, KC])
    g2col_d = din("g2col", [128, KC])
    gqb_d = din("gqb", [128, 128])
    gkb_d = din("gkb", [128, 128])
    tpadR_d = din("tpadR", [NH, 128 * 384])
    ch_d = din("ch", [128, NH])
    lngb_d = din("lngb", [128, 128])
    lnbb_d = din("lnbb", [128, 128])
    ws_d = din("ws", [128, NH, 128])
    bsb_d = din("bsb", [128, NH, 128])
    convw_d = din("convw", [128, NF, 3])
    convb_d = din("convb", [128, NF])

    yp = dout("yp", [SEQ, D])
    ys = dout("ys", [128, D])
    nkp = dout("nkp", [512, 1024])
    nvp = dout("nvp", [512, 1024])
    ncp = dout("ncp", [2, DFF])
    nks = dout("nks", [128, 1024])
    nvs = dout("nvs", [128, 1024])
    ngs = dout("ngs", [128, 1024])
    ncs = dout("ncs", [2, 2, DFF])

    NBLK = 48
    scr = nc.dram_tensor("scr", [NBLK, 128, KC * 512], BF16, kind="Internal").ap()

    es = ExitStack()
    P = Prog(nc, es)
    E = es.enter_context

    def sb(name, shape, dt=F32):
        return E(nc.sbuf_tensor("s_" + name, list(shape), dt))

    NXH = 3
    xh = Ring(nc, es, "xh", [128, D], F32, NXH)
    kT = sb("kT", [128, NH, 9 * 128], BF16)
    kTb = [Buf("kT%d" % i) for i in range(9)]
    Va = sb("Va", [128, 9, NH, 130], BF16)
    Vab = [Buf("Va%d" % i) for i in range(9)]
    BT = sb("BT", [128, NH, 256])
    BTb = Buf("BT")
    chT = sb("chT", [128, NH]); chb = Buf("ch")
    bsb = sb("bsb", [128, NH, 128]); bsbb = Buf("bsb")
    wsT = sb("wsT", [128, NH, 128], BF16); wsTb = Buf("wsT")
    wsTS = sb("wsTS", [128, NH, 128], BF16); wsTSb = Buf("wsTS")
    gqb = sb("gqb", [128, 128]); gqbb = Buf("gqb")
    gkb = sb("gkb", [128, 128]); gkbb = Buf("gkb")
    lngb = sb("lngb", [128, 128]); lngbb = Buf("lngb")
    lnbb = sb("lnbb", [128, 128]); lnbbb = Buf("lnbb")
    g1col = sb("g1col", [128, KC]); g1b = Buf("g1col")
    g2col = sb("g2col", [128, KC]); g2b = Buf("g2col")
    convw = sb("convw", [128, NF, 3]); convwb = Buf("convw")
    convb = sb("convb", [128, NF]); convbb = Buf("convb")
    ident = sb("ident", [128, 128], BF16); identb = Buf("ident")
    identf = sb("identf", [128, 128]); identfb = Buf("identf")
    mhalf = sb("mhalf", [128, 4]); mhalfb = Buf("mhalf")
    mku = sb("mku", [1, 128], BF16); mkv = sb("mkv", [1, 128], BF16); mkb = Buf("mkuv")
    halo = sb("halo", [128, NF, 2]); halob = Buf("halo")
    haloS = sb("haloS", [128, 2, NF, 2]); haloSb = Buf("haloS")
    aconv = sb("aconv", [128, 3, 2, NF]); aconvb = Buf("aconv")

    NSLOT = 3
    wslot = [sb("wslot%d" % i, [128, KC, 512], BF16) for i in range(NSLOT)]
    wslotb = [Buf("wslot%d" % i) for i in range(NSLOT)]
    scrb = [Buf("scr%d" % i) for i in range(NBLK)]

    nmix = sb("nmix", [128, KC, T], BF16)
    nmixb = [[Buf("nmix%d_%d" % (t, hf)) for hf in range(2)] for t in range(NT)]

    arena = sb("arena", [128, NF * T], BF16)
    mT = arena[:, :].rearrange("p (f t) -> p f t", t=T)
    mTb = [Buf("mT%d" % f) for f in range(NF)]
    qT = arena[:, 0:NH * T].rearrange("p (h t) -> p h t", t=T)
    qTb = [Buf("qT%d" % t) for t in range(NT)]
    uT = arena[:, NH * T:2 * NH * T].rearrange("p (c t) -> p c t", t=T)
    uTb = Buf("uT")
    vn = arena[:, 2 * NH * T:2 * NH * T + NT * 1024].rearrange("p (t c) -> p t c", c=1024)
    vnb = [Buf("vn%d" % t) for t in range(NT)]
    for f in range(NF):
        lo, hi = f * T, (f + 1) * T
        al = []
        if lo < NH * T:
            al += qTb
        if lo < 2 * NH * T and hi > NH * T:
            al.append(uTb)
        if lo < 2 * NH * T + NT * 1024 and hi > 2 * NH * T:
            al += vnb
        for x in al:
            mTb[f].al.append(x)
            x.al.append(mTb[f])

    nbR = Ring(nc, es, "nb", [128, D], BF16, 1)
    ss1R = Ring(nc, es, "ss1", [128, 4], F32, 6)
    rs1R = Ring(nc, es, "rs1", [128, 4], F32, 6)
    qnR = Ring(nc, es, "qn", [128, 512], BF16, 2)
    _w, wsT2b = qnR.next()
    wsT2 = _w[:, :].rearrange("p (g t) -> p g t", t=64)
    st32R = Ring(nc, es, "st32", [128, 512], F32, 2)
    scr8 = sb("scr8", [128, D])
    xtmp = scr8[:, :]
    xtmpb = Buf("xtmp")
    _gvb = [Buf("gv0"), Buf("gv1")]
    _gtb = Buf("gtmp0")
    gvR = Ring.views([(scr8[:, 1024:1536], _gvb[0]), (scr8[:, 1536:2048], _gvb[1])])
    for _b in _gvb + [_gtb]:
        _b.al.append(xtmpb)
        xtmpb.al.append(_b)
    bnsR = Ring(nc, es, "bns", [128, 4, 6], F32, 3)
    bnmR = Ring(nc, es, "bnm", [128, 4, 2], F32, 3)
    stmpR = Ring(nc, es, "stmp", [128, 256], F32, 3)
    PTR = Ring(nc, es, "PT", [128, 640], BF16, 3)
    recR = Ring(nc, es, "rec", [128, 4], F32, 4)
    oaR = Ring(nc, es, "oa", [128, 1024], BF16, 2)
    gtmpR = Ring.views([(scr8[:, 0:1024], _gtb)])
    _gt1 = Buf("gtmp1")
    for _b in _gvb + [xtmpb]:
        _b.al.append(_gt1)
        _gt1.al.append(_b)
    cstR = Ring.views([(scr8[:, 0:1024], _gtb), (scr8[:, 1024:2048], _gt1)])
    accR = Ring(nc, es, "acc", [128, T], F32, 2)

    psum = [E(nc.psum_tensor("ps%d" % i, [128, 512], F32)) for i in range(8)]
    psumb = [Buf("ps%d" % i) for i in range(8)]
    for _b in psumb:
        _b.excl = True
    pctr = [0]

    def bank():
        k = pctr[0] % 8
        pctr[0] += 1
        return psum[k], psumb[k]

    def load_const(t, src, b):
        P.dma("sp", t[:], src, w=[b])

    load_const(g1col, g1col_d, g1b)
    load_const(g2col, g2col_d, g2b)
    load_const(gqb, gqb_d, gqbb)
    load_const(gkb, gkb_d, gkbb)
    load_const(chT, ch_d, chb)
    load_const(lngb, lngb_d, lngbb)
    load_const(lnbb, lnbb_d, lnbbb)
    load_const(bsb, bsb_d, bsbb)
    load_const(convw, convw_d, convwb)
    load_const(convb, convb_d, convbb)
    for h in range(NH):
        src = bass.AP(tensor=tpadR_d.tensor, offset=h * 128 * 384 + 128, ap=[[383, 128], [1, 256]])
        P.dma("sp", BT[:, h, :], src, w=[BTb])
    P.op("pool", lambda: nc.gpsimd.memset(BT[64:128, :, 0:64], NEG), w=[BTb])
    P.op("pool", lambda: nc.gpsimd.memset(mhalf[:], -0.5), w=[mhalfb])
    P.op("pool", lambda: nc.gpsimd.memset(mku[:], 0.0), w=[mkb])
    P.op("pool", lambda: nc.gpsimd.memset(mku[:, 0:64], 1.0), w=[mkb])
    P.op("pool", lambda: nc.gpsimd.memset(mkv[:], 0.0), w=[mkb])
    P.op("pool", lambda: nc.gpsimd.memset(mkv[:, 64:128], NEG), w=[mkb])
    P.op("pool", lambda: nc.gpsimd.memset(identf[:], 1.0), w=[identfb])
    P.op("pool", lambda: nc.gpsimd.affine_select(out=identf[:], in_=identf[:], pattern=[[-1, 128]],
                                                 compare_op=ALU.is_equal, fill=0.0, base=0,
                                                 channel_multiplier=1), r=[identfb], w=[identfb])
    P.op("pool", lambda: nc.gpsimd.tensor_copy(out=ident[:], in_=identf[:]), r=[identfb], w=[identb])
    P.op("pool", lambda: nc.gpsimd.memset(halo[:], 0.0), w=[halob])
    for _k in range(len(PTR.t)):
        P.op("pool", (lambda _k=_k: nc.gpsimd.memset(PTR.t[_k][:], 0.0)), w=[PTR.b[_k]])
    for s in range(9):
        P.op("pool", (lambda s=s: nc.gpsimd.memset(Va[:, s, :, 128:129], 1.0)), w=[Vab[s]])
    _w, wsmb = gtmpR.next()
    wsm = _w[:, :].rearrange("p (g s) -> p g s", s=128)
    _w, wsm16b = oaR.next()
    wsm16 = _w[:, :].rearrange("p (g s) -> p g s", s=128)
    _w, wsm16sb = oaR.next()
    wsm16s = _w[0:64, :].rearrange("p (g s) -> p g s", s=128)
    P.dma("sp", wsm, ws_d, w=[wsmb])
    P.op("pool", lambda: nc.gpsimd.affine_select(out=wsm, in_=wsm, pattern=[[0, NH], [-1, 128]],
                                                 compare_op=ALU.is_ge, fill=0.0, base=0,
                                                 channel_multiplier=1), r=[wsmb], w=[wsmb])
    P.op("pool", lambda: nc.gpsimd.tensor_copy(out=wsm16, in_=wsm), r=[wsmb], w=[wsm16b])
    P.op("pool", lambda: nc.gpsimd.memset(wsm16s, 0.0), w=[wsm16sb])
    P.op("pool", lambda: nc.gpsimd.tensor_copy(out=wsm16s[:, :, 64:128], in_=wsm[0:64, :, 0:64]),
         r=[wsmb], w=[wsm16sb])
    for half in range(2):
        pt, ptb = bank()
        ptv = pt[:].bitcast(BF16)
        for j in range(4):
            g = half * 4 + j
            P.op("pe", (lambda g=g, j=j, ptv=ptv: nc.tensor.transpose(
                out=ptv[:, j * 128:(j + 1) * 128], in_=wsm16[:, g, :], identity=ident[:])),
                r=[wsm16b, identb], w=[ptb])
        P.op("act", (lambda half=half, ptv=ptv: nc.scalar.copy(
            out=wsT[:, half * 4:(half + 1) * 4, :],
            in_=ptv[:, 0:512].rearrange("p (j t) -> p j t", t=128))), r=[ptb], w=[wsTb])
    pt, ptb = bank()
    ptv = pt[:].bitcast(BF16)
    for g in range(NH):
        P.op("pe", (lambda g=g, ptv=ptv: nc.tensor.transpose(
            out=ptv[:, g * 64:(g + 1) * 64], in_=wsm16s[:, g, :], identity=ident[0:64, 0:64])),
            r=[wsm16sb, identb], w=[ptb])
    P.op("act", (lambda ptv=ptv: nc.scalar.copy(
        out=wsT2[64:128, :, :], in_=ptv[64:128, 0:512].rearrange("p (g t) -> p g t", t=64))),
        r=[ptb], w=[wsT2b])
    P.op("pool", lambda: nc.gpsimd.memset(wsTS[:], 0.0), w=[wsTSb])
    P.op("pool", lambda: nc.gpsimd.tensor_copy(out=wsTS[0:64, :, 0:64], in_=wsT[0:64, :, 0:64]),
         r=[wsTb], w=[wsTSb])
    P.op("pool", lambda: nc.gpsimd.tensor_copy(out=wsTS[64:128, :, 64:128], in_=wsT2[64:128, :, :]),
         r=[wsT2b], w=[wsTSb])

    PB = blocks_of_pass()
    NCONV = max(1, min(NCONV_MAX, NPASS - 1))
    assert len(PB) == NBLK
    wstate = {"next": 0}
    TOTAL = NPASS * NBLK

    def emit_load(gi):
        p, bi = divmod(gi, NBLK)
        blk = PB[bi]
        s = gi % NSLOT
        t, b = wslot[s], wslotb[s]
        cp = bi % NCONV
        if p <= cp:
            q = "pool"
            P.dma(q, t[:].rearrange("p k n -> p (k n)"), wblk[bi], w=[b])
            if p == cp and p < NPASS - 1:
                P.dma("sp", scr[bi], t[:].rearrange("p k n -> p (k n)"), r=[b], w=[scrb[bi]], key=b)
        else:
            P.dma("sp", t[:].rearrange("p k n -> p (k n)"), scr[bi], r=[scrb[bi]], w=[b], key=b)

    def wget(p, bi):
        gi = p * NBLK + bi
        while wstate["next"] <= min(gi + NSLOT - 1, TOTAL - 1):
            emit_load(wstate["next"])
            wstate["next"] += 1
        s = gi % NSLOT
        return wslot[s], wslotb[s]

    def rms_rstd(ss_ap, ssb, n_inv, width):
        rs, rsb = rs1R.next()
        P.op("dve", lambda: nc.vector.tensor_scalar(out=rs[:, 0:width], in0=ss_ap, scalar1=n_inv, scalar2=EPS,
                                                    op0=ALU.mult, op1=ALU.add), r=[ssb], w=[rsb])
        P.op("pool", lambda: nc.gpsimd.tensor_tensor(out=rs[:, 0:width], in0=rs[:, 0:width],
                                                     in1=mhalf[:, 0:width], op=ALU.pow),
             r=[rsb, mhalfb], w=[rsb])
        return rs, rsb

    def norm_chain(xt, xb):
        ss, ssb = ss1R.next()
        nb_, nbb = nbR.next()
        P.op("act", lambda: nc.scalar.activation(out=nb_[:], in_=xt[:], func=AF.Square, accum_out=ss[:, 0:1]),
             r=[xb], w=[nbb, ssb])
        rs, rsb = rms_rstd(ss[:, 0:1], ssb, 1.0 / D, 1)
        P.op("dve", lambda: nc.vector.tensor_scalar(out=nb_[:], in0=xt[:], scalar1=rs[:, 0:1], scalar2=None,
                                                    op0=ALU.mult), r=[xb, rsb], w=[nbb])
        return nb_, nbb

    def norm_chain_pre(xt, xb, ssp, sspb):
        rs0, rs0b = rs1R.next()
        P.op("dve", lambda: nc.vector.tensor_reduce(out=rs0[:, 0:1], in_=ssp[:, 0:4], axis=mybir.AxisListType.X,
                                                    op=ALU.add), r=[sspb], w=[rs0b])
        rs, rsb = rms_rstd(rs0[:, 0:1], rs0b, 1.0 / D, 1)
        nb_, nbb = nbR.next()
        P.op("dve", lambda: nc.vector.tensor_scalar(out=nb_[:, 0:D // 2], in0=xt[:, 0:D // 2], scalar1=rs[:, 0:1],
                                                    scalar2=None, op0=ALU.mult), r=[xb, rsb], w=[nbb])
        P.op("act", lambda: nc.scalar.activation(out=nb_[:, D // 2:D], in_=xt[:, D // 2:D], func=AF.Copy,
                                                 scale=rs[:, 0:1]), r=[xb, rsb], w=[nbb])
        return nb_, nbb

    def norm_T(nbp, t, gcol, gb):
        nb_, nbb = nbp
        for hf in range(2):
            pt, ptb = bank()
            ptv = pt[:].bitcast(BF16)
            for j in range(8):
                kc = hf * 8 + j
                P.op("pe", (lambda kc=kc, j=j, ptv=ptv: nc.tensor.transpose(
                    out=ptv[:, j * 128:(j + 1) * 128], in_=nb_[:, kc * 128:(kc + 1) * 128], identity=ident[:])),
                    r=[nbb, identb], w=[ptb])
            for j in range(8):
                kc = hf * 8 + j
                if hf == 0:
                    P.op("act", (lambda kc=kc, j=j, ptv=ptv: nc.scalar.activation(
                        out=nmix[:, kc, t * 128:(t + 1) * 128], in_=ptv[:, j * 128:(j + 1) * 128],
                        func=AF.Copy, scale=gcol[:, kc:kc + 1])), r=[ptb, gb], w=[nmixb[t][hf]])
                else:
                    P.op("dve", (lambda kc=kc, j=j, ptv=ptv: nc.vector.tensor_scalar(
                        out=nmix[:, kc, t * 128:(t + 1) * 128], in0=ptv[:, j * 128:(j + 1) * 128],
                        scalar1=gcol[:, kc:kc + 1], scalar2=None, op0=ALU.mult)), r=[ptb, gb], w=[nmixb[t][hf]])

    def norm_to_T(xt, xb, t, gcol, gb):
        norm_T(norm_chain(xt, xb), t, gcol, gb)

    def sample_attention(i, oa, oab):
        for s in range(2):
            for t in range(4):
                sl = s * 4 + t
                ckt, cktb = cstR.next()
                P.dma("sp", ckt[:], ck[s, t * 128:(t + 1) * 128, :], w=[cktb])
                for half in range(2):
                    pt, ptb = bank()
                    for j in range(4):
                        h = half * 4 + j
                        P.op("pe", (lambda pt=pt, j=j, h=h, ckt=ckt: nc.tensor.transpose(
                            out=pt[:, j * 128:(j + 1) * 128], in_=ckt[:, h * 128:(h + 1) * 128],
                            identity=identf[:])), r=[cktb, identfb], w=[ptb])
                    P.op("act", (lambda pt=pt, half=half, sl=sl: nc.scalar.copy(
                        out=kT[:, half * 4:(half + 1) * 4, sl * 128:(sl + 1) * 128],
                        in_=pt[:].rearrange("p (j t) -> p j t", t=128))), r=[ptb], w=[kTb[sl]])
                cvt, cvtb = cstR.next()
                P.dma("sp", cvt[:], cv[s, t * 128:(t + 1) * 128, :], w=[cvtb])
                P.op("dve", (lambda cvt=cvt, sl=sl: nc.vector.tensor_copy(
                    out=Va[:, sl, :, 0:128], in_=cvt[:].rearrange("p (h c) -> p h c", c=128))),
                    r=[cvtb], w=[Vab[sl]])
        stA, stB = [], []
        hstate = {}
        for h in range(NH):
            for s in range(2):
                ust = {}

                def A(h=h, s=s, ust=ust):
                    p0 = s * 64
                    q0 = i * 128 + s * 64
                    psA, psAb = bank()
                    psB, psBb = bank()
                    for t in range(4):
                        sl = s * 4 + t
                        if t == 3:
                            dps, dpb, c0 = psB, psBb, 64
                        else:
                            dps, dpb, c0 = psA, psAb, (2 - t) * 64
                        P.op("pe", (lambda dps=dps, c0=c0, sl=sl: nc.tensor.matmul(
                            dps[:, c0:c0 + 64], lhsT=kT[:, h, sl * 128:(sl + 1) * 128], rhs=qT[:, h, q0:q0 + 64],
                            start=True, stop=True)), r=[kTb[sl], qTb[i]], w=[dpb])
                    P.op("pe", (lambda: nc.tensor.matmul(
                        psB[p0:p0 + 64, 0:64], lhsT=kT[:, h, 8 * 128 + p0:8 * 128 + p0 + 64],
                        rhs=qT[:, h, q0:q0 + 64], start=True, stop=True)), r=[kTb[8], qTb[i]], w=[psBb])
                    stmp, stb = stmpR.next()
                    PT, PTb = PTR.next()
                    ust["PT"] = (PT, PTb)
                    P.op("dve", (lambda: nc.vector.scalar_tensor_tensor(
                        out=stmp[:, 64:128], in0=psB[:, 64:128], scalar=SCALE, in1=BT[:, h, 128:192],
                        op0=ALU.mult, op1=ALU.add)), r=[psBb, BTb], w=[stb])
                    P.op("dve", (lambda: nc.vector.scalar_tensor_tensor(
                        out=stmp[p0:p0 + 64, 0:64], in0=psB[p0:p0 + 64, 0:64], scalar=SCALE,
                        in1=BT[p0:p0 + 64, h, p0:p0 + 64], op0=ALU.mult, op1=ALU.add)), r=[psBb, BTb], w=[stb])
                    P.op("act", (lambda: nc.scalar.activation(
                        out=PT[:, 64:128], in_=stmp[:, 64:128], func=AF.Exp)), r=[stb], w=[PTb])
                    P.op("act", (lambda: nc.scalar.activation(
                        out=PT[p0:p0 + 64, 0:64], in_=stmp[p0:p0 + 64, 0:64], func=AF.Exp)), r=[stb], w=[PTb])
                    P.op("act", (lambda: nc.scalar.activation(
                        out=PT[64 - p0:128 - p0, 0:64], in_=BT[64 - p0:128 - p0, h, 64:128], func=AF.Copy,
                        scale=0.0)), r=[BTb], w=[PTb])
                    P.op("act", (lambda: nc.scalar.activation(
                        out=PT[:, 256:448], in_=psA[:, 0:192], func=AF.Exp, bias=chT[:, h:h + 1], scale=SCALE)),
                        r=[psAb, chb], w=[PTb])

                def B(h=h, s=s, ust=ust):
                    p0 = s * 64
                    PT, PTb = ust["PT"]
                    if s == 0:
                        hstate[h] = bank()
                    psO, psOb = hstate[h]
                    order = [(3, 64), (2, 256), (1, 320), (0, 384)]
                    for n_, (t, c0) in enumerate(order):
                        sl = s * 4 + t
                        P.op("pe", (lambda c0=c0, sl=sl, n_=n_: nc.tensor.matmul(
                            psO[p0:p0 + 64, 0:129], lhsT=PT[:, c0:c0 + 64], rhs=Va[:, sl, h, 0:129],
                            start=(n_ == 0), stop=False)), r=[PTb, Vab[sl]], w=[psOb])
                    P.op("pe", (lambda: nc.tensor.matmul(
                        psO[p0:p0 + 64, 0:129], lhsT=PT[:, 0:64], rhs=Va[:, 8, h, 0:129],
                        start=False, stop=True)), r=[PTb, Vab[8]], w=[psOb])
                    if s == 1:
                        rec, recb = recR.next()
                        P.op("dve", (lambda: nc.vector.reciprocal(
                            out=rec[:, 0:1], in_=psO[:, 128:129])), r=[psOb], w=[recb])
                        P.op("dve", (lambda: nc.vector.tensor_scalar(
                            out=oa[:, h * 128:(h + 1) * 128], in0=psO[:, 0:128], scalar1=rec[:, 0:1], scalar2=None,
                            op0=ALU.mult)), r=[psOb, recb], w=[oab])

                stA.append(A)
                stB.append((B, i if (h == NH - 1 and s == 1) else None))
        return stA, stB

    for p in range(NPASS if STOP > 0 else 0):
        last = (p == NPASS - 1)
        tiles = []
        for i in range(NT):
            g = p * NT + i
            tiles.append(("p", g) if g < NP else ("s",))
        xt_l = []
        for i, td in enumerate(tiles):
            xt, xb = xh.next()
            src = xp[td[1] * 128:(td[1] + 1) * 128, :] if td[0] == "p" else xs
            P.dma("sp", xt[:], src, w=[xb])
            xt_l.append((xt, xb))
        if p == 0 or not PREFETCH:
            for i in range(NT):
                norm_to_T(xt_l[i][0], xt_l[i][1], i, g1col, g1b)
        nall = [nmixb[t][hf] for t in range(NT) for hf in range(2)]

        if p >= STOPP and STOP < 2:
            break
        deferred = []

        def tick(flush=False):
            keep_ = []
            for ent in deferred:
                ent[0] -= 1
                if ent[0] <= 0 or flush:
                    ent[1]()
                else:
                    keep_.append(ent)
            deferred[:] = keep_

        for bi_, nb in enumerate(S2ORDER[:NBMAX]):
            wt, wb = wget(p, bi_)
            if nb in (6, 7):
                for c in range(4):
                    ps, psb = bank()
                    for kc in range(KC):
                        P.op("pe", (lambda ps=ps, wt=wt, kc=kc, c=c: nc.tensor.matmul(
                            ps[:, 0:T], lhsT=wt[:, kc, c * 128:(c + 1) * 128], rhs=nmix[:, kc, :],
                            start=(kc == 0), stop=(kc == KC - 1))), r=[wb] + nall, w=[psb])
                    cc = (nb - 6) * 4 + c
                    P.op("act", (lambda ps=ps, cc=cc: nc.scalar.activation(
                        out=uT[:, cc, :], in_=ps[:, 0:T], func=AF.Gelu)), r=[psb], w=[uTb])
                    tick()
                continue
            for i, td in enumerate(tiles):
                ps, psb = bank()
                for kc in range(KC):
                    P.op("pe", (lambda ps=ps, wt=wt, kc=kc, i=i: nc.tensor.matmul(
                        ps[:], lhsT=nmix[:, kc, i * 128:(i + 1) * 128], rhs=wt[:, kc, :],
                        start=(kc == 0), stop=(kc == KC - 1))), r=[wb, nmixb[i][0], nmixb[i][1]], w=[psb])
                tick()
                is_s = td[0] == "s"
                gslot = (td[1] % 8) if not is_s else None
                keep = is_s or td[1] >= KEEP0
                if nb < 4:
                    isq = nb < 2
                    h0 = (nb % 2) * 4
                    ss, ssb = ss1R.next()
                    qn, qnb = qnR.next()
                    for j in range(4):
                        P.op("act", (lambda ps=ps, j=j, ss=ss, qn=qn: nc.scalar.activation(
                            out=qn[:, j * 128:(j + 1) * 128], in_=ps[:, j * 128:(j + 1) * 128], func=AF.Square,
                            accum_out=ss[:, j:j + 1])), r=[psb], w=[qnb, ssb])
                    rs, rsb = rms_rstd(ss[:, 0:4], ssb, 1.0 / 128, 4)
                    gt, gtb = (gqb, gqbb) if isq else (gkb, gkbb)
                    for j in range(4):
                        P.op("dve", (lambda ps=ps, j=j, rs=rs, qn=qn, gt=gt: nc.vector.scalar_tensor_tensor(
                            out=qn[:, j * 128:(j + 1) * 128], in0=ps[:, j * 128:(j + 1) * 128],
                            scalar=rs[:, j:j + 1], in1=gt[:], op0=ALU.mult, op1=ALU.mult)),
                            r=[psb, rsb, gtb], w=[qnb])
                    if (not isq) and keep:
                        s32, s32b = st32R.next()
                        for j in range(4):
                            P.op("dve", (lambda ps=ps, j=j, rs=rs, s32=s32, gt=gt: nc.vector.scalar_tensor_tensor(
                                out=s32[:, j * 128:(j + 1) * 128], in0=ps[:, j * 128:(j + 1) * 128],
                                scalar=rs[:, j:j + 1], in1=gt[:], op0=ALU.mult, op1=ALU.mult)),
                                r=[psb, rsb, gtb], w=[s32b])
                        if is_s:
                            dst = nks[:, h0 * 128:(h0 + 4) * 128]
                        else:
                            r0 = (td[1] - KEEP0) * 128
                            dst = nkp[r0:r0 + 128, h0 * 128:(h0 + 4) * 128]
                        P.dma("sp", dst, s32[:], r=[s32b])
                    def qk_T(qn=qn, qnb=qnb, isq=isq, h0=h0, i=i, is_s=is_s, gslot=gslot):
                        pt, ptb = bank()
                        ptv = pt[:].bitcast(BF16)
                        for j in range(4):
                            P.op("pe", (lambda j=j: nc.tensor.transpose(
                                out=ptv[:, j * 128:(j + 1) * 128], in_=qn[:, j * 128:(j + 1) * 128],
                                identity=ident[:])), r=[qnb, identb], w=[ptb])
                        if isq:
                            P.op("act", (lambda: nc.scalar.copy(
                                out=qT[:, h0:h0 + 4, i * 128:(i + 1) * 128],
                                in_=ptv[:, 0:512].rearrange("p (j t) -> p j t", t=128))), r=[ptb], w=[qTb[i]])
                        else:
                            ks = gslot if not is_s else 8
                            P.op("act", (lambda: nc.scalar.copy(
                                out=kT[:, h0:h0 + 4, ks * 128:(ks + 1) * 128],
                                in_=ptv[:, 0:512].rearrange("p (j t) -> p j t", t=128))), r=[ptb],
                                w=[kTb[ks]])
                    deferred.append([2, qk_T])
                elif nb < 6:
                    h0 = (nb - 4) * 4
                    vs_ = gslot if not is_s else 8
                    vbuf = Vab[vs_]
                    if VMODE & 1:
                        P.op("act", (lambda ps=ps, h0=h0, vs_=vs_: nc.scalar.copy(
                            out=Va[:, vs_, h0:h0 + 4, 0:128],
                            in_=ps[:].rearrange("p (j t) -> p j t", t=128))), r=[psb], w=[vbuf])
                    if keep and (VMODE & 2):
                        s32, s32b = st32R.next()
                        P.op("dve", (lambda ps=ps, s32=s32: nc.vector.tensor_copy(out=s32[:], in_=ps[:])),
                             r=[psb], w=[s32b])
                        if is_s:
                            dst = nvs[:, h0 * 128:(h0 + 4) * 128]
                        else:
                            r0 = (td[1] - KEEP0) * 128
                            dst = nvp[r0:r0 + 128, h0 * 128:(h0 + 4) * 128]
                        P.dma("sp", dst, s32[:], r=[s32b])
                else:
                    g0 = (nb - 8) * 4
                    gv, gvb = gvR.next()
                    P.op("act", (lambda ps=ps, gv=gv: nc.scalar.activation(out=gv[:], in_=ps[:], func=AF.Gelu)),
                         r=[psb], w=[gvb])
                    bs_, bsb_ = bnsR.next()
                    bm_, bmb_ = bnmR.next()
                    for j in range(4):
                        P.op("dve", (lambda j=j, gv=gv, bs_=bs_: nc.vector.bn_stats(
                            out=bs_[:, j, :], in_=gv[:, j * 128:(j + 1) * 128])), r=[gvb], w=[bsb_])
                    for j in range(4):
                        P.op("dve", (lambda j=j, bs_=bs_, bm_=bm_: nc.vector.bn_aggr(
                            out=bm_[:, j, :], in_=bs_[:, j, :])), r=[bsb_], w=[bmb_])
                    rs, rsb = rs1R.next()
                    P.op("dve", (lambda rs=rs, bm_=bm_: nc.vector.tensor_scalar(
                        out=rs[:, 0:4], in0=bm_[:, :, 1], scalar1=EPS, scalar2=None, op0=ALU.add)),
                        r=[bmb_], w=[rsb])
                    P.op("pool", (lambda rs=rs: nc.gpsimd.tensor_tensor(
                        out=rs[:, 0:4], in0=rs[:, 0:4], in1=mhalf[:, 0:4], op=ALU.pow)),
                        r=[rsb, mhalfb], w=[rsb])
                    for j in range(4):
                        P.op("dve", (lambda j=j, gv=gv, bm_=bm_, rs=rs: nc.vector.tensor_scalar(
                            out=gv[:, j * 128:(j + 1) * 128], in0=gv[:, j * 128:(j + 1) * 128],
                            scalar1=bm_[:, j, 0:1], scalar2=rs[:, j:j + 1], op0=ALU.subtract, op1=ALU.mult)),
                            r=[gvb, bmb_, rsb], w=[gvb])
                    gv3 = gv[:].rearrange("p (j c) -> p j c", c=128)
                    P.op("pool", (lambda gv3=gv3: nc.gpsimd.tensor_tensor(
                        out=gv3, in0=gv3, in1=lngb[:].unsqueeze(1).to_broadcast([128, 4, 128]), op=ALU.mult)),
                        r=[gvb, lngbb], w=[gvb])
                    if is_s:
                        P.op("pool", (lambda gv3=gv3: nc.gpsimd.tensor_tensor(
                            out=gv3, in0=gv3, in1=lnbb[:].unsqueeze(1).to_broadcast([128, 4, 128]), op=ALU.add)),
                            r=[gvb, lnbbb], w=[gvb])
                        P.op("pool", (lambda gv=gv, i=i, g0=g0: nc.gpsimd.tensor_copy(
                            out=vn[:, i, g0 * 128:(g0 + 4) * 128], in_=gv[:])), r=[gvb], w=[vnb[i]])
                        P.dma("sp", ngs[:, g0 * 128:(g0 + 4) * 128], gv[:], r=[gvb])
                    else:
                        P.op("pool", (lambda gv3=gv3, i=i, g0=g0: nc.gpsimd.tensor_tensor(
                            out=vn[:, i, g0 * 128:(g0 + 4) * 128].rearrange("p (j c) -> p j c", c=128), in0=gv3,
                            in1=lnbb[:].unsqueeze(1).to_broadcast([128, 4, 128]), op=ALU.add)),
                            r=[gvb, lnbbb], w=[vnb[i]])

        tick(flush=True)
        if p >= STOPP and STOP < 3:
            break
        stageA, stageB = [], []
        oas = {}
        for i, td in enumerate(tiles):
            if td[0] != "p":
                continue
            oa, oab = oaR.next()
            oas[i] = (oa, oab)
            g = td[1]
            navail = min(5, g + 1)
            tstate = {"psO": None}
            for h in range(NH):
                ust = {}

                def A(i=i, g=g, h=h, navail=navail, ust=ust):
                    psA, psAb = bank()
                    psB, psBb = bank()
                    for tr in range(4, 4 - navail, -1):
                        gk = g - 4 + tr
                        ksl = gk % 8
                        if tr >= 3:
                            dstps, dstb, c0 = psB, psBb, (4 - tr) * 128
                        else:
                            dstps, dstb, c0 = psA, psAb, (2 - tr) * 128
                        P.op("pe", (lambda dstps=dstps, c0=c0, ksl=ksl, tr=tr: nc.tensor.matmul(
                            dstps[:, c0:c0 + 128], lhsT=kT[:, h, ksl * 128:(ksl + 1) * 128],
                            rhs=qT[:, h, i * 128:(i + 1) * 128], start=True, stop=(tr != 0))),
                            r=[kTb[ksl], qTb[i]], w=[dstb])
                        if tr == 0:
                            P.op("pe", (lambda dstps=dstps, c0=c0: nc.tensor.matmul(
                                dstps[:, c0:c0 + 128], lhsT=mku[:, :], rhs=mkv[:, :], start=False, stop=True)),
                                r=[mkb], w=[dstb])
                    nB = min(navail, 2)
                    nA = navail - nB
                    stmp, stb = stmpR.next()
                    PT, PTb = PTR.next()
                    ust["PT"] = (PT, PTb)
                    P.op("dve", (lambda: nc.vector.scalar_tensor_tensor(
                        out=stmp[:, 0:nB * 128], in0=psB[:, 0:nB * 128], scalar=SCALE, in1=BT[:, h, 0:nB * 128],
                        op0=ALU.mult, op1=ALU.add)), r=[psBb, BTb], w=[stb])
                    P.op("act", (lambda: nc.scalar.activation(
                        out=PT[:, 0:nB * 128], in_=stmp[:, 0:nB * 128], func=AF.Exp)), r=[stb], w=[PTb])
                    if nA > 0:
                        P.op("act", (lambda: nc.scalar.activation(
                            out=PT[:, 256:256 + nA * 128], in_=psA[:, 0:nA * 128], func=AF.Exp,
                            bias=chT[:, h:h + 1], scale=SCALE)), r=[psAb, chb], w=[PTb])

                def B(i=i, g=g, h=h, navail=navail, ust=ust, tstate=tstate, oa=oa, oab=oab):
                    PT, PTb = ust["PT"]
                    if h % 3 == 0:
                        tstate["psO"] = bank()
                        tstate["h0"] = h
                    psO, psOb = tstate["psO"]
                    jo = (h % 3) * 160
                    cnt = 0
                    for tr in range(4, 4 - navail, -1):
                        gk = g - 4 + tr
                        ksl = gk % 8
                        c0 = (4 - tr) * 128 if tr >= 3 else 256 + (2 - tr) * 128
                        P.op("pe", (lambda c0=c0, ksl=ksl, cnt=cnt: nc.tensor.matmul(
                            psO[:, jo:jo + 129], lhsT=PT[:, c0:c0 + 128], rhs=Va[:, ksl, h, 0:129],
                            start=(cnt == 0), stop=(cnt == navail - 1))), r=[PTb, Vab[ksl]], w=[psOb])
                        cnt += 1
                    if h % 3 == 2 or h == NH - 1:
                        hb0 = tstate["h0"]
                        nh_ = h - hb0 + 1
                        rec, recb = recR.next()
                        P.op("dve", (lambda: nc.vector.reciprocal(
                            out=rec[:, 0:nh_],
                            in_=psO[:, 0:160 * nh_].rearrange("p (j c) -> p j c", c=160)[:, :, 128])),
                            r=[psOb], w=[recb])
                        P.op("dve", (lambda: nc.vector.tensor_tensor(
                            out=oa[:, hb0 * 128:(hb0 + nh_) * 128].rearrange("p (j c) -> p j c", c=128),
                            in0=psO[:, 0:160 * nh_].rearrange("p (j c) -> p j c", c=160)[:, :, 0:128],
                            in1=rec[:, 0:nh_].unsqueeze(2).to_broadcast([128, nh_, 128]), op=ALU.mult)),
                            r=[psOb, recb], w=[oab])

                stageA.append(A)
                stageB.append((B, i if h == NH - 1 else None))

        def oa_transposes(i):
            oa, oab = oas[i]
            pt, ptb = bank()
            ptv = pt[:].bitcast(BF16)
            for h in range(NH):
                P.op("pe", (lambda h=h: nc.tensor.transpose(
                    out=ptv[:, h * 128:(h + 1) * 128], in_=oa[:, h * 128:(h + 1) * 128], identity=ident[:])),
                    r=[oab, identb], w=[ptb])
            P.op("act", (lambda: nc.scalar.copy(
                out=nmix[:, 0:8, i * 128:(i + 1) * 128],
                in_=ptv[:, 0:1024].rearrange("p (j t) -> p j t", t=128))), r=[ptb], w=[nmixb[i][0]])

        def run_pipeline(stA_, stB_, LAG=2):
            pend_T = []
            nU = len(stA_)
            for k in range(nU + LAG):
                if k < nU:
                    stA_[k]()
                if k >= LAG:
                    Bf, tdone = stB_[k - LAG]
                    Bf()
                    if tdone is not None:
                        pend_T.append((k + 2, tdone))
                while pend_T and pend_T[0][0] <= k:
                    oa_transposes(pend_T.pop(0)[1])
            for _, ti in pend_T:
                oa_transposes(ti)

        run_pipeline(stageA, stageB)
        for i, td in enumerate(tiles):
            if td[0] == "s":
                oa, oab = oaR.next()
                oas[i] = (oa, oab)
                sA, sB = sample_attention(i, oa, oab)
                run_pipeline(sA, sB)
        if p >= STOPP and STOP < 4:
            break
        for i, td in enumerate(tiles):
            gt_, gtb_ = gtmpR.next()
            for half in range(2):
                ps, psb = bank()
                for j in range(4):
                    g = half * 4 + j
                    if td[0] == "p":
                        P.op("pe", (lambda ps=ps, j=j, g=g, i=i: nc.tensor.matmul(
                            ps[:, j * 128:(j + 1) * 128], lhsT=vn[:, i, g * 128:(g + 1) * 128], rhs=wsT[:, g, :],
                            start=True, stop=True)), r=[vnb[i], wsTb], w=[psb])
                    else:
                        P.op("pe", (lambda ps=ps, j=j, g=g, i=i: nc.tensor.matmul(
                            ps[:, j * 128:(j + 1) * 128], lhsT=vn[:, i, g * 128:(g + 1) * 128], rhs=wsTS[:, g, :],
                            start=True, stop=True)), r=[vnb[i], wsTSb], w=[psb])
                if td[0] == "p":
                    P.op("dve", (lambda ps=ps, gt_=gt_, half=half: nc.vector.tensor_tensor(
                        out=gt_[:, half * 512:(half + 1) * 512], in0=ps[:],
                        in1=bsb[:, half * 4:(half + 1) * 4, :].rearrange("p g t -> p (g t)"), op=ALU.add)),
                        r=[psb, bsbb], w=[gtb_])
                else:
                    for sq in range(2):
                        P.op("dve", (lambda ps=ps, gt_=gt_, half=half, sq=sq: nc.vector.tensor_tensor(
                            out=gt_[:, half * 512:(half + 1) * 512].rearrange(
                                "p (g t) -> p g t", t=128)[:, :, sq * 64:(sq + 1) * 64],
                            in0=ps[:].rearrange("p (g t) -> p g t", t=128)[:, :, sq * 64:(sq + 1) * 64],
                            in1=bsb[:, half * 4:(half + 1) * 4, 0:64], op=ALU.add)),
                            r=[psb, bsbb], w=[gtb_])
            P.op("pool", (lambda gt_=gt_, i=i: nc.gpsimd.tensor_tensor(
                out=nmix[:, 8:16, i * 128:(i + 1) * 128],
                in0=gt_[:].rearrange("p (g t) -> p g t", t=128),
                in1=uT[:, :, i * 128:(i + 1) * 128], op=ALU.mult)), r=[gtb_, uTb], w=[nmixb[i][1]])

        if p >= STOPP and STOP < 5:
            break
        ssps = [ss1R.next() for _ in range(NT)]
        for nb in range(4):
            wt, wb = wget(p, 10 + nb)
            for i in range(NT):
                ps, psb = bank()
                xt, xb = xt_l[i]
                for kc in range(KC):
                    P.op("pe", (lambda ps=ps, wt=wt, kc=kc, i=i: nc.tensor.matmul(
                        ps[:], lhsT=nmix[:, kc, i * 128:(i + 1) * 128], rhs=wt[:, kc, :],
                        start=(kc == 0), stop=(kc == KC - 1))), r=[wb, nmixb[i][0], nmixb[i][1]], w=[psb])
                P.op("dve", (lambda ps=ps, xt=xt, nb=nb: nc.vector.tensor_tensor(
                    out=xt[:, nb * 512:(nb + 1) * 512], in0=ps[:], in1=xt[:, nb * 512:(nb + 1) * 512],
                    op=ALU.add)), r=[psb, xb], w=[xb])
                jq, jqb = qnR.next()
                P.op("act", (lambda xt=xt, nb=nb, jq=jq, i=i: nc.scalar.activation(
                    out=jq[:], in_=xt[:, nb * 512:(nb + 1) * 512], func=AF.Square,
                    accum_out=ssps[i][0][:, nb:nb + 1])), r=[xb], w=[jqb, ssps[i][1]])
                if nb == 3:
                    if i >= 1:
                        norm_T(n2pend, i - 1, g2col, g2b)
                    n2pend = norm_chain_pre(xt, xb, ssps[i][0], ssps[i][1])
        norm_T(n2pend, NT - 1, g2col, g2b)
        if p >= STOPP and STOP < 6:
            break
        nall = [nmixb[t][hf] for t in range(NT) for hf in range(2)]

        if p >= STOPP and STOP < 7:
            break
        segs = []
        c = 0
        npr = sum(1 for td in tiles if td[0] == "p")
        if npr:
            segs.append((0, npr * 128, "prev" if p > 0 else None))
        if last:
            segs.append((npr * 128, npr * 128 + 64, "sA"))
            segs.append((npr * 128 + 64, npr * 128 + 128, "sB"))
        if last:
            P.dma("sp", haloS[:, 0], cst[0], w=[haloSb])
            P.dma("sp", haloS[:, 1], cst[1], w=[haloSb])
        for j in range(22):
            wt, wb = wget(p, 14 + j)
            nch = 2 if j < 21 else 1
            for cch in range(nch):
                f = 2 * j + cch
                psa, psab = bank()
                psg, psgb = bank()
                for kc in range(KC):
                    P.op("pe", (lambda psa=psa, wt=wt, kc=kc, cch=cch: nc.tensor.matmul(
                        psa[:, 0:T], lhsT=wt[:, kc, cch * 128:(cch + 1) * 128], rhs=nmix[:, kc, :],
                        start=(kc == 0), stop=(kc == KC - 1))), r=[wb] + nall, w=[psab])
                goff = nch * 128
                for kc in range(KC):
                    P.op("pe", (lambda psg=psg, wt=wt, kc=kc, cch=cch, goff=goff: nc.tensor.matmul(
                        psg[:, 0:T], lhsT=wt[:, kc, goff + cch * 128:goff + (cch + 1) * 128], rhs=nmix[:, kc, :],
                        start=(kc == 0), stop=(kc == KC - 1))), r=[wb] + nall, w=[psgb])
                acc, accb = accR.next()
                P.op("act", (lambda psa=psa, acc=acc, f=f: nc.scalar.activation(
                    out=acc[:], in_=psa[:, 0:T], func=AF.Identity, scale=convw[:, f, 2:3],
                    bias=convb[:, f:f + 1])), r=[psab, convwb, convbb], w=[accb])
                for (c0, c1, hs) in segs:
                    P.op("dve", (lambda psa=psa, acc=acc, f=f, c0=c0, c1=c1: nc.vector.scalar_tensor_tensor(
                        out=acc[:, c0 + 1:c1], in0=psa[:, c0:c1 - 1], scalar=convw[:, f, 1:2],
                        in1=acc[:, c0 + 1:c1], op0=ALU.mult, op1=ALU.add)), r=[psab, convwb, accb], w=[accb])
                    P.op("dve", (lambda psa=psa, acc=acc, f=f, c0=c0, c1=c1: nc.vector.scalar_tensor_tensor(
                        out=acc[:, c0 + 2:c1], in0=psa[:, c0:c1 - 2], scalar=convw[:, f, 0:1],
                        in1=acc[:, c0 + 2:c1], op0=ALU.mult, op1=ALU.add)), r=[psab, convwb, accb], w=[accb])
                    if hs is not None:
                        if hs == "prev":
                            hap, hb_ = halo[:, f, :], halob
                        elif hs == "sA":
                            hap, hb_ = haloS[:, 0, f, :], haloSb
                        else:
                            hap, hb_ = haloS[:, 1, f, :], haloSb
                        P.op("dve", (lambda acc=acc, f=f, c0=c0, hap=hap: nc.vector.scalar_tensor_tensor(
                            out=acc[:, c0:c0 + 2], in0=hap, scalar=convw[:, f, 0:1],
                            in1=acc[:, c0:c0 + 2], op0=ALU.mult, op1=ALU.add)), r=[hb_, convwb, accb], w=[accb])
                        P.op("dve", (lambda acc=acc, f=f, c0=c0, hap=hap: nc.vector.scalar_tensor_tensor(
                            out=acc[:, c0:c0 + 1], in0=hap[:, 1:2], scalar=convw[:, f, 1:2],
                            in1=acc[:, c0:c0 + 1], op0=ALU.mult, op1=ALU.add)), r=[hb_, convwb, accb], w=[accb])
                if not last:
                    P.op("dve", (lambda psa=psa, f=f: nc.vector.tensor_copy(
                        out=halo[:, f, :], in_=psa[:, T - 2:T])), r=[psab], w=[halob])
                else:
                    e0 = npr * 128 - 2
                    P.op("dve", (lambda psa=psa, f=f, e0=e0: nc.vector.tensor_copy(
                        out=aconv[:, 0, :, f], in_=psa[:, e0:e0 + 2])), r=[psab], w=[aconvb])
                    P.op("dve", (lambda psa=psa, f=f, e0=e0: nc.vector.tensor_copy(
                        out=aconv[:, 1:3, :, f],
                        in_=psa[:, e0 + 64:e0 + 192].rearrange("p (s c) -> p s c", c=64)[:, :, 0:2])),
                        r=[psab], w=[aconvb])
                P.op("act", (lambda acc=acc: nc.scalar.activation(out=acc[:], in_=acc[:], func=AF.Silu)),
                     r=[accb], w=[accb])
                P.op("dve", (lambda acc=acc, psg=psg, f=f: nc.vector.tensor_tensor(
                    out=mT[:, f, :], in0=acc[:], in1=psg[:, 0:T], op=ALU.mult)), r=[accb, psgb], w=[mTb[f]])

        if p >= STOPP and STOP < 8:
            break
        for ob in range(4):
            pbs = [bank() for _ in range(NT)]
            for kg in range(3):
                wt, wb = wget(p, 36 + ob * 3 + kg)
                nfl = 16 if kg < 2 else 11
                for i in range(NT):
                    ps, psb = pbs[i]
                    for fl in range(nfl):
                        f = kg * 16 + fl
                        P.op("pe", (lambda ps=ps, wt=wt, fl=fl, f=f, i=i: nc.tensor.matmul(
                            ps[:], lhsT=mT[:, f, i * 128:(i + 1) * 128], rhs=wt[:, fl, :],
                            start=(f == 0), stop=(f == NF - 1))), r=[wb, mTb[f]], w=[psb])
                if kg == 0 and PREFETCH and not last:
                    if ob >= 1:
                        norm_T(n1pend, ob - 1, g1col, g1b)
                    if ob < NT:
                        gnext = (p + 1) * NT + ob
                        srcn = xp[gnext * 128:(gnext + 1) * 128, :] if gnext < NP else xs
                        P.dma("sp", xtmp, srcn, w=[xtmpb])
                        n1pend = norm_chain(xtmp, xtmpb)
            for i, td in enumerate(tiles):
                ps, psb = pbs[i]
                xt, xb = xt_l[i]
                P.op("dve", (lambda ps=ps, xt=xt, ob=ob: nc.vector.tensor_tensor(
                    out=xt[:, ob * 512:(ob + 1) * 512], in0=ps[:], in1=xt[:, ob * 512:(ob + 1) * 512],
                    op=ALU.add)), r=[psb, xb], w=[xb])
                dst = (yp[td[1] * 128:(td[1] + 1) * 128, ob * 512:(ob + 1) * 512] if td[0] == "p"
                       else ys[:, ob * 512:(ob + 1) * 512])
                P.dma("sp", dst, xt[:, ob * 512:(ob + 1) * 512], r=[xb], key=xb)

        if p >= STOPP and STOP < 9:
            break
        if last:
            for s3 in range(3):
                pt, ptb = bank()
                P.op("pe", (lambda pt=pt, s3=s3: nc.tensor.transpose(
                    out=pt[0:2 * NF, 0:128], in_=aconv[:, s3, :, :].rearrange("p r f -> p (r f)"),
                    identity=identf[:])), r=[aconvb, identfb], w=[ptb])
                aT, aTb = st32R.next()
                P.op("act", (lambda pt=pt, aT=aT: nc.scalar.copy(out=aT[0:2 * NF, 0:128], in_=pt[0:2 * NF, 0:128])),
                     r=[ptb], w=[aTb])
                for r_ in range(2):
                    dst = (ncp[r_] if s3 == 0 else ncs[s3 - 1, r_]).rearrange("(c p) -> c p", p=128)
                    P.dma("sp", dst, aT[r_ * NF:(r_ + 1) * NF, 0:128], r=[aTb])

    stats = P.finalize()
    return nc, es, stats


_CACHE = {}


def _get_prog(NP):
    if NP not in _CACHE:
        _CACHE[NP] = build(NP)
    return _CACHE[NP]


def _block_major_weights(w_in, w_out, w_up, w_down):
    out = np.zeros((48, 128, KC, 512), np.float32)
    for bi, blk in enumerate(blocks_of_pass()):
        if blk[0] == "in":
            nb = blk[1]
            out[bi] = w_in[:, nb * 512:(nb + 1) * 512].reshape(KC, 128, 512).transpose(1, 0, 2)
        elif blk[0] == "out":
            nb = blk[1]
            out[bi] = w_out[:, nb * 512:(nb + 1) * 512].reshape(KC, 128, 512).transpose(1, 0, 2)
        elif blk[0] == "up":
            j = blk[1]
            n = 256 if j < 21 else 128
            out[bi, :, :, 0:n] = w_up[:, j * 256:j * 256 + n].reshape(KC, 128, n).transpose(1, 0, 2)
            out[bi, :, :, n:2 * n] = w_up[:, DFF + j * 256:DFF + j * 256 + n].reshape(KC, 128, n).transpose(1, 0, 2)
        else:
            ob, kg = blk[1], blk[2]
            nfl = 16 if kg < 2 else 11
            out[bi, :, 0:nfl, :] = w_down[kg * 2048:kg * 2048 + nfl * 128, ob * 512:(ob + 1) * 512].reshape(
                nfl, 128, 512).transpose(1, 0, 2)
    return out.reshape(48, 128, KC * 512)


def _permute_w_up(w):
    parts = []
    for j in range(22):
        n = 256 if j < 21 else 128
        parts.append(w[:, j * 256:j * 256 + n])
        parts.append(w[:, DFF + j * 256:DFF + j * 256 + n])
    return np.ascontiguousarray(np.concatenate(parts, axis=1))


def make_in_maps(inp, n_cores):
    f = lambda a: np.ascontiguousarray(np.asarray(a, dtype=np.float32))
    x_prompt = f(inp["x_prompt"]); x_sample = f(inp["x_sample"])
    ck = f(inp["cache_attn_k"])[0]; cv = f(inp["cache_attn_v"])[0]; cst = f(inp["state_ffn_conv"])[0]
    table = f(inp["rel_bias_table"])[0]
    tpad = np.concatenate([table, np.repeat(table[:, 256:257], 127, axis=1)], axis=1)
    tpadR = f(np.repeat(tpad[:, None, :], 128, axis=1).reshape(NH, 128 * 384))
    shared = {
        "wblk": _block_major_weights(f(inp["w_in"])[0], f(inp["w_out"])[0], f(inp["w_up"])[0],
                                     f(inp["w_down"])[0]),
        "g1col": f(f(inp["norm_mix_g"])[0].reshape(KC, 128).T),
        "g2col": f(f(inp["norm_ffn_g"])[0].reshape(KC, 128).T),
        "gqb": f(np.broadcast_to(f(inp["q_norm_g"])[0][None, :], (128, 128))),
        "gkb": f(np.broadcast_to(f(inp["k_norm_g"])[0][None, :], (128, 128))),
        "tpadR": tpadR,
        "ch": f(np.broadcast_to(table[:, 256][None, :], (128, NH))),
        "lngb": f(np.broadcast_to(f(inp["gmlp_ln_g"])[0][None, :], (128, 128))),
        "lnbb": f(np.broadcast_to(f(inp["gmlp_ln_b"])[0][None, :], (128, 128))),
        "ws": f(f(inp["gmlp_w_s"])[0].transpose(1, 0, 2)),
        "bsb": f(np.broadcast_to(f(inp["gmlp_b_s"])[0][None, :, :], (128, NH, 128))),
        "convw": f(f(inp["ffn_conv_w"])[0].reshape(3, NF, 128).transpose(2, 1, 0)),
        "convb": f(f(inp["ffn_conv_b"])[0].reshape(NF, 128).T),
    }
    maps = []
    for c in range(n_cores):
        m = dict(shared)
        m["xp"] = x_prompt[c]
        m["xs"] = f(x_sample[2 * c:2 * c + 2].reshape(128, D))
        m["ck"] = f(ck[2 * c:2 * c + 2].reshape(2, 512, 1024))
        m["cv"] = f(cv[2 * c:2 * c + 2].reshape(2, 512, 1024))
        m["cst"] = f(cst[2 * c:2 * c + 2].reshape(2, 2, NF, 128).transpose(0, 3, 2, 1))
        maps.append(m)
    return maps


def assemble(results, n_cores, SEQ):
    B, DB = n_cores, 2 * n_cores
    y_p = np.empty((B, SEQ, D), np.float32)
    y_s = np.empty((DB, 64, D), np.float32)
    nk_p = np.empty((1, B, 512, NH, 128), np.float32)
    nv_p = np.empty((1, B, 512, NH, 128), np.float32)
    nc_p = np.empty((1, B, 2, DFF), np.float32)
    nk_s = np.empty((1, DB, 64, NH, 128), np.float32)
    nv_s = np.empty((1, DB, 64, NH, 128), np.float32)
    ng_s = np.empty((1, DB, 64, 1024), np.float32)
    nc_s = np.empty((1, DB, 2, DFF), np.float32)
    for c in range(n_cores):
        r = results[c]
        y_p[c] = r["yp"]
        y_s[2 * c:2 * c + 2] = r["ys"].reshape(2, 64, D)
        nk_p[0, c] = r["nkp"].reshape(512, NH, 128)
        nv_p[0, c] = r["nvp"].reshape(512, NH, 128)
        nc_p[0, c] = r["ncp"]
        nk_s[0, 2 * c:2 * c + 2] = r["nks"].reshape(2, 64, NH, 128)
        nv_s[0, 2 * c:2 * c + 2] = r["nvs"].reshape(2, 64, NH, 128)
        ng_s[0, 2 * c:2 * c + 2] = r["ngs"].reshape(2, 64, 1024)
        nc_s[0, 2 * c:2 * c + 2] = r["ncs"]
    return (y_p, y_s, nk_p, nv_p, nc_p, nk_s, nv_s, ng_s, nc_s)


def kernel(**inputs):
    SEQ = int(np.asarray(inputs["x_prompt"]).shape[1])
    n_cores = int(np.asarray(inputs["x_prompt"]).shape[0])
    nc, _es, _stats = _get_prog(SEQ // 128)
    maps = make_in_maps(inputs, n_cores)
    res = run_bass_kernel_spmd(nc, maps, core_ids=list(range(n_cores)))
    return assemble(res.results, n_cores, SEQ)
```
